# Optimizing a Trainium2 kernel written in Bass

```python
import jax, jax.numpy as jnp
from jax import lax
import numpy as np

D_MODEL = 1024
BATCH = 8
SEQ = 2048
DEPTH = 2
DEC_BATCH = 128
DEC_SEQ = 1
PAST_LEN = 8192
PAGE_SIZE = 128

N_A_LAYERS = DEPTH // 2
N_B_LAYERS = DEPTH - N_A_LAYERS
CHUNK = 128
A_WIDTH = 2 * D_MODEL
A_GROUPS = 8
A_GROUP_DIM = A_WIDTH // A_GROUPS
HEAD_DIM = 64
N_HEADS = D_MODEL // HEAD_DIM
N_KV_HEADS = 4
GQA_GROUP = N_HEADS // N_KV_HEADS
WINDOW = 128
Q_BLOCK = 128
ROT_DIM = HEAD_DIM // 4
ROPE_THETA = 500000.0
EPS = 1e-5
BUF_LEN = min(WINDOW, PAST_LEN)

kernel_name = "yoco_chunk_gmlp_swa_sink_step"


def rms_norm(x, g):
    xf = x.astype(jnp.float32)
    y = xf * lax.rsqrt(jnp.mean(xf * xf, axis=-1, keepdims=True) + EPS)
    return (y * g.astype(jnp.float32)).astype(x.dtype)


def rotary(x, start):
    L = x.shape[1]
    pos = (start + jnp.arange(L)).astype(jnp.float32)
    inv = ROPE_THETA ** (-jnp.arange(0, ROT_DIM, 2, dtype=jnp.float32) / ROT_DIM)
    ang = pos[:, None] * inv[None, :]
    cos = jnp.cos(ang)[None, :, None, :]
    sin = jnp.sin(ang)[None, :, None, :]
    xr = x[..., :ROT_DIM].astype(jnp.float32)
    x1, x2 = xr[..., :ROT_DIM // 2], xr[..., ROT_DIM // 2:]
    rot = jnp.concatenate([x1 * cos - x2 * sin, x2 * cos + x1 * sin], axis=-1).astype(x.dtype)
    return jnp.concatenate([rot, x[..., ROT_DIM:]], axis=-1)


def chunk_gmlp_mixer(h, norm_g, w_in, v_norm_g, w_s, b_s, w_out):
    B, L, _ = h.shape
    xn = rms_norm(h, norm_g)
    proj = jnp.einsum('bld,de->ble', xn, w_in)
    u, v, gate = jnp.split(proj, 3, axis=-1)
    v = rms_norm(v, v_norm_g)
    cl = CHUNK if L >= CHUNK else L
    n = -(-L // cl)
    pad = n * cl - L
    vp = jnp.pad(v, ((0, 0), (0, pad), (0, 0))).reshape(B, n, cl, A_GROUPS, A_GROUP_DIM)
    causal = jnp.tril(jnp.ones((cl, cl), dtype=bool))
    ws = jnp.where(causal[None], w_s[:, :cl, :cl], 0)
    z = jnp.einsum('gij,bnjgc->bnigc', ws, vp) + b_s[:, :cl].T[None, None, :, :, None]
    z = z.reshape(B, n * cl, A_WIDTH)[:, :L]
    y = u * z * jax.nn.silu(gate)
    return jnp.einsum('ble,ed->bld', y, w_out), v


def shared_kv(h, kv_norm, w_kv, start):
    B, L, _ = h.shape
    xn = rms_norm(h, kv_norm)
    kv = jnp.einsum('bld,de->ble', xn, w_kv)
    k, v = jnp.split(kv, 2, axis=-1)
    k = rotary(k.reshape(B, L, N_KV_HEADS, HEAD_DIM), start)
    return k, v.reshape(B, L, N_KV_HEADS, HEAD_DIM)


def sliding_window_attention(q, k_ext, v_ext, sinks, start):
    B, L = q.shape[:2]
    qb = Q_BLOCK if L >= Q_BLOCK else L
    n = -(-L // qb)
    pad = n * qb - L
    q = jnp.pad(q, ((0, 0), (0, pad), (0, 0), (0, 0))).reshape(B, n, qb, N_KV_HEADS, GQA_GROUP, HEAD_DIM)
    k_ext = jnp.pad(k_ext, ((0, 0), (0, pad), (0, 0), (0, 0)))
    v_ext = jnp.pad(v_ext, ((0, 0), (0, pad), (0, 0), (0, 0)))
    span = WINDOW + qb
    idx = (jnp.arange(n) * qb)[:, None] + jnp.arange(span)[None, :]
    kb = k_ext[:, idx]
    vb = v_ext[:, idx]
    s = jnp.einsum('bnqkgd,bnskd->bnkgqs', q, kb, preferred_element_type=jnp.float32) * (HEAD_DIM ** -0.5)
    i = jnp.arange(qb)[:, None]
    j = jnp.arange(span)[None, :]
    band = (j >= i) & (j <= WINDOW + i)
    key_pos = start - WINDOW + idx
    valid = band[None] & (key_pos >= 0)[:, None, :]
    s = jnp.where(valid[None, :, None, None], s, -jnp.inf)
    sink = jnp.broadcast_to(sinks.astype(jnp.float32).reshape(1, 1, N_KV_HEADS, GQA_GROUP, 1, 1), s.shape[:-1] + (1,))
    p = jax.nn.softmax(jnp.concatenate([s, sink], axis=-1), axis=-1)[..., :-1]
    o = jnp.einsum('bnkgqs,bnskd->bnqkgd', p.astype(v_ext.dtype), vb)
    return o.reshape(B, n * qb, N_HEADS * HEAD_DIM)[:, :L]


def swa_mixer(h, k_ext, v_ext, start, norm_g, w_in, sinks, w_out):
    B, L, _ = h.shape
    xn = rms_norm(h, norm_g)
    proj = jnp.einsum('bld,de->ble', xn, w_in)
    q, gate = jnp.split(proj, 2, axis=-1)
    q = rotary(q.reshape(B, L, N_HEADS, HEAD_DIM), start)
    o = sliding_window_attention(q, k_ext, v_ext, sinks, start)
    return jnp.einsum('ble,ed->bld', o * jax.nn.silu(gate), w_out)


def trunk(x, start, k_past, v_past, norm_a, w_in_a, v_norm_a, w_s_a, b_s_a, w_out_a,
          kv_norm, w_kv, norm_b, w_in_b, sinks_b, w_out_b, final_norm):
    B, L, _ = x.shape
    front = WINDOW - k_past.shape[1]
    h = x
    a_rows = []
    k_ext = None
    v_ext = None
    for layer in range(DEPTH):
        if layer < N_A_LAYERS:
            out, v_rows = chunk_gmlp_mixer(h, norm_a[layer], w_in_a[layer], v_norm_a[layer],
                                           w_s_a[layer], b_s_a[layer], w_out_a[layer])
            h = h + out
            a_rows.append(v_rows)
        else:
            if layer == N_A_LAYERS:
                k_new, v_new = shared_kv(h, kv_norm, w_kv, start)
                k_ext = jnp.concatenate([jnp.pad(k_past, ((0, 0), (front, 0), (0, 0), (0, 0))).astype(k_new.dtype), k_new], axis=1)
                v_ext = jnp.concatenate([jnp.pad(v_past, ((0, 0), (front, 0), (0, 0), (0, 0))).astype(v_new.dtype), v_new], axis=1)
            lb = layer - N_A_LAYERS
            h = h + swa_mixer(h, k_ext, v_ext, start, norm_b[lb], w_in_b[lb], sinks_b[lb], w_out_b[lb])
    y = rms_norm(h, final_norm)
    keep = min(WINDOW, start + L)
    return y, k_ext[:, -keep:], v_ext[:, -keep:], jnp.stack(a_rows)


def setup_inputs(seed: int = 0) -> dict:
    key = jax.random.key(seed)
    ks = jax.random.split(key, 20)
    f32 = jnp.float32
    nrm = lambda k, shape, scale: jax.random.normal(k, shape, f32) * scale
    return {
        'x_prompt': nrm(ks[0], (BATCH, SEQ, D_MODEL), 1.0),
        'x_sample': nrm(ks[1], (DEC_BATCH, DEC_SEQ, D_MODEL), 1.0),
        'cache_k': nrm(ks[2], (DEC_BATCH, BUF_LEN, N_KV_HEADS, HEAD_DIM), 1.0),
        'cache_v': nrm(ks[3], (DEC_BATCH, BUF_LEN, N_KV_HEADS, HEAD_DIM), 1.0),
        'norm_a': 1.0 + nrm(ks[4], (N_A_LAYERS, D_MODEL), 0.1),
        'w_in_a': nrm(ks[5], (N_A_LAYERS, D_MODEL, 3 * A_WIDTH), D_MODEL ** -0.5),
        'v_norm_a': 1.0 + nrm(ks[6], (N_A_LAYERS, A_WIDTH), 0.1),
        'w_s_a': nrm(ks[7], (N_A_LAYERS, A_GROUPS, CHUNK, CHUNK), CHUNK ** -0.5),
        'b_s_a': 1.0 + nrm(ks[8], (N_A_LAYERS, A_GROUPS, CHUNK), 0.1),
        'w_out_a': nrm(ks[9], (N_A_LAYERS, A_WIDTH, D_MODEL), A_WIDTH ** -0.5),
        'kv_norm': 1.0 + nrm(ks[10], (D_MODEL,), 0.1),
        'w_kv': nrm(ks[11], (D_MODEL, 2 * N_KV_HEADS * HEAD_DIM), D_MODEL ** -0.5),
        'norm_b': 1.0 + nrm(ks[12], (N_B_LAYERS, D_MODEL), 0.1),
        'w_in_b': nrm(ks[13], (N_B_LAYERS, D_MODEL, 2 * N_HEADS * HEAD_DIM), D_MODEL ** -0.5),
        'sinks_b': nrm(ks[14], (N_B_LAYERS, N_HEADS), 1.0),
        'w_out_b': nrm(ks[15], (N_B_LAYERS, N_HEADS * HEAD_DIM, D_MODEL), (N_HEADS * HEAD_DIM) ** -0.5),
        'final_norm': 1.0 + nrm(ks[16], (D_MODEL,), 0.1),
    }


def reference(x_prompt, x_sample, cache_k, cache_v, norm_a, w_in_a, v_norm_a, w_s_a, b_s_a, w_out_a,
              kv_norm, w_kv, norm_b, w_in_b, sinks_b, w_out_b, final_norm):
    weights = (norm_a, w_in_a, v_norm_a, w_s_a, b_s_a, w_out_a, kv_norm, w_kv, norm_b, w_in_b, sinks_b, w_out_b, final_norm)
    empty = jnp.zeros((x_prompt.shape[0], 0, N_KV_HEADS, HEAD_DIM), x_prompt.dtype)
    y_prompt, new_k_prompt, new_v_prompt, _ = trunk(x_prompt, 0, empty, empty, *weights)
    y_sample, new_k_sample, new_v_sample, new_av_sample = trunk(x_sample, PAST_LEN, cache_k, cache_v, *weights)
    return (y_prompt, y_sample, new_k_prompt, new_v_prompt, new_k_sample, new_v_sample, new_av_sample)
```

```python
import contextlib
import numpy as np
import concourse.bass as bass
import concourse.mybir as mybir
from concourse.bass_utils import run_bass_kernel_spmd

F32 = mybir.dt.float32
BF16 = mybir.dt.bfloat16
AF = mybir.ActivationFunctionType
ALU = mybir.AluOpType
AX = mybir.AxisListType

NCORES = 8
D = 1024
SEQ = 2048
NT = SEQ // 128
NS = 16
AW = 2048
EPS = 1e-5
NEG = -30000.0


class Eng:
    def __init__(self, name, h, sem):
        self.name = name
        self.h = h
        self.sem = sem
        self.cnt = 0
        self.seen = {}


class T:
    def __init__(self, name, ap=None):
        self.name = name
        self.ap = ap
        self.w = None
        self.r = {}
        self.dsem = None
        self.dsem_sw = None
        self.psum = False

    def __getitem__(self, k):
        return self.ap[k]


class TV:
    def __init__(self, base, ap):
        object.__setattr__(self, "base", base)
        object.__setattr__(self, "ap", ap)

    def __getattr__(self, k):
        return getattr(object.__getattribute__(self, "base"), k)

    def __setattr__(self, k, v):
        setattr(object.__getattribute__(self, "base"), k, v)

    def __getitem__(self, k):
        return object.__getattribute__(self, "ap")[k]


class FW:
    def __init__(self, nc, es):
        self.nc = nc
        self.es = es
        self.nsem = 0
        mk = lambda n, h: Eng(n, h, self.new_sem(n))
        self.pe = mk("pe", nc.tensor)
        self.act = mk("act", nc.scalar)
        self.dve = mk("dve", nc.vector)
        self.pool = mk("pool", nc.gpsimd)
        self.sp = mk("sp", nc.sync)
        self.engs = [self.pe, self.act, self.dve, self.pool, self.sp]
        self.dma_holders = []
        self.muted = False

    def new_sem(self, name):
        self.nsem += 1
        return self.es.enter_context(self.nc.semaphore(f"s{self.nsem}_{name}"))

    def sb(self, name, shape, dt, dma=False, es=None):
        t = (es or self.es).enter_context(self.nc.sbuf_tensor("sb_" + name, list(shape), dt))
        tt = T(name, t)
        if dma:
            self.add_dsem(tt)
        return tt

    def add_dsem(self, tt):
        pass

    def holder(self, tt, issuer):
        attr = "dsem_sw" if issuer is self.pool else "dsem"
        h = getattr(tt, attr, None)
        if h is None:
            h = Eng(attr + "_" + tt.name, None, self.new_sem("d"))
            setattr(tt, attr, h)
            self.dma_holders.append(h)
        return h

    def ps(self, name, shape, dt, es=None):
        t = (es or self.es).enter_context(self.nc.psum_tensor("ps_" + name, list(shape), dt))
        tt = T(name, t)
        tt.psum = True
        return tt

    def _deps(self, comp, reads, writes):
        deps = {}

        def add(h, v):
            if deps.get(h, 0) < v:
                deps[h] = v
        for t in reads:
            if t.w is not None:
                add(*t.w)
            if t.psum:
                for h, v in t.r.items():
                    if h is not comp:
                        add(h, v)
        for t in writes:
            if t.w is not None:
                add(*t.w)
            for h, v in t.r.items():
                add(h, v)
        return deps

    def _wait(self, issuer, comp, deps):
        for h, v in deps.items():
            if h is self.pe and comp is self.pe:
                continue
            if issuer.seen.get(h, 0) < v:
                issuer.h.wait_ge(h.sem, v)
                issuer.seen[h] = v

    def _commit(self, comp, reads, writes, inc):
        comp.cnt += inc
        for t in reads:
            t.r[comp] = comp.cnt
        for t in writes:
            t.w = (comp, comp.cnt)
            t.r = {}

    def op(self, eng, fn, reads=(), writes=()):
        if self.muted:
            return None
        deps = self._deps(eng, reads, writes)
        self._wait(eng, eng, deps)
        ins = fn()
        ins.then_inc(eng.sem, 1)
        self._commit(eng, reads, writes, 1)
        return ins

    def group(self, fns, reads=(), writes=()):
        if self.muted:
            return None
        eng = self.pe
        deps = self._deps(eng, reads, writes)
        self._wait(eng, eng, deps)
        ins = None
        for f in fns:
            ins = f()
        ins.then_inc(eng.sem, 1)
        self._commit(eng, reads, writes, 1)

    def dma(self, issuer, out, in_, holder, reads=(), writes=()):
        if self.muted:
            return None
        comp = self.holder(holder, issuer)
        deps = self._deps(comp, reads, writes)
        self._wait(issuer, comp, deps)
        ins = issuer.h.dma_start(out=out, in_=in_)
        ins.then_inc(comp.sem, 16)
        self._commit(comp, reads, writes, 16)
        return ins

    def barrier_all(self):
        if self.muted:
            return None
        holders = self.engs + self.dma_holders
        for e in self.engs:
            for h in holders:
                if h is e:
                    continue
                if h.cnt > 0 and e.seen.get(h, 0) < h.cnt:
                    e.h.wait_ge(h.sem, h.cnt)
                    e.seen[h] = h.cnt


class StopBuild(Exception):
    pass


KSTOP = [None]
KSKIP = set()


FWREF = [None]


def ckpt(name):
    if KSTOP[0] == name:
        FWREF[0].muted = True


def build_program():
    nc = bass.Bass("TRN2", target_bir_lowering=False)

    def din(name, shape):
        return nc.dram_tensor(name, list(shape), F32, kind="ExternalInput").ap()

    def dout(name, shape):
        return nc.dram_tensor(name, list(shape), F32, kind="ExternalOutput").ap()

    xp = din("xp", [SEQ, D])
    xsm = din("xsm", [NS, D])
    ck = din("ck", [NS, 128, 256])
    cv = din("cv", [NS, 128, 256])
    w_in_a = din("w_in_a", [D, 3 * AW])
    w_out_a = din("w_out_a", [AW, D])
    w_kv = din("w_kv", [D, 512])
    w_in_b = din("w_in_b", [D, 2048])
    w_out_b = din("w_out_b", [D, D])
    gaT_d = din("gaT", [128, 8])
    gvT_d = din("gvT", [128, 16])
    gkvT_d = din("gkvT", [128, 8])
    gbT_d = din("gbT", [128, 8])
    gv_row = din("gv_row", [AW])
    gf_row = din("gf_row", [D])
    wsT_d = din("wsT", [128, 8 * 128])
    w00_d = din("w00", [8])
    bs_d = din("bs", [8 * 128])
    bs0_d = din("bs0", [8])
    sinks_d = din("sinks", [16])
    rope_d = din("rope", [SEQ + 128, 16])
    ident_d = din("ident", [128, 128])
    cmask_d = din("cmask", [128, 128])
    mprev_d = din("mprev", [128, 512])
    mcur_d = din("mcur", [128, 512])

    yp = dout("yp", [SEQ, D])
    ysm = dout("ysm", [NS, D])
    nkp = dout("nkp", [128, 256])
    nvp = dout("nvp", [128, 256])
    nks = dout("nks", [NS, 128, 256])
    nvs = dout("nvs", [NS, 128, 256])
    nav = dout("nav", [NS, AW])
    h1s = nc.dram_tensor("h1s", [SEQ + 128, D], F32).ap()

    with contextlib.ExitStack() as es:
        fw = FW(nc, es)
        FWREF[0] = fw
        pe, act, dve, pool, sp = fw.pe, fw.act, fw.dve, fw.pool, fw.sp
        V, S, G, P_ = nc.vector, nc.scalar, nc.gpsimd, nc.tensor

        def rstd_from_ss(ss_ap, tmp_t, out_t, n, width):
            fw.op(dve, lambda: V.tensor_scalar(out=tmp_t[0:n, 0:1], in0=ss_ap, scalar1=1.0 / width, scalar2=EPS,
                                               op0=ALU.mult, op1=ALU.add), reads=[tmp_t.src], writes=[tmp_t])
            fw.op(act, lambda: S.activation(out=tmp_t[0:n, 0:1], in_=tmp_t[0:n, 0:1], func=AF.Ln),
                  reads=[tmp_t], writes=[tmp_t])
            fw.op(act, lambda: S.activation(out=out_t[0:n, 0:1], in_=tmp_t[0:n, 0:1], func=AF.Exp, scale=-0.5),
                  reads=[tmp_t], writes=[out_t])

        def body():
            identb = fw.sb("identb", [128, 128], BF16, dma=True)
            fw.dma(pool, identb[:], ident_d[:, :], identb, writes=[identb])
            gaT = fw.sb("gaT", [128, 8], F32, dma=True)
            fw.dma(sp, gaT[:], gaT_d[:, :], gaT, writes=[gaT])
            gvT = fw.sb("gvT", [128, 16], F32, dma=True)
            fw.dma(sp, gvT[:], gvT_d[:, :], gvT, writes=[gvT])
            gkvT = fw.sb("gkvT", [128, 8], F32, dma=True)
            fw.dma(sp, gkvT[:], gkvT_d[:, :], gkvT, writes=[gkvT])
            gbT = fw.sb("gbT", [128, 8], F32, dma=True)
            fw.dma(sp, gbT[:], gbT_d[:, :], gbT, writes=[gbT])
            onesf = fw.sb("onesf", [128, 8], F32)
            fw.op(pool, lambda: G.memset(onesf[:], 1.0), writes=[onesf])
            ssx = fw.sb("ssx", [128, 1], F32)
            tmx = fw.sb("tmx", [128, 1], F32)
            rsx = fw.sb("rsx", [128, 1], F32)
            tmx.src = ssx
            ssv = fw.sb("ssv", [128, 4], F32)
            ssv1 = fw.sb("ssv1", [128, 1], F32)
            tmv = fw.sb("tmv", [128, 1], F32)
            rsv = fw.sb("rsv", [128, 1], F32)
            tmv.src = ssv1

            WSH = [fw.sb(f"WSH{i}", [128, 8, 512], BF16, dma=True) for i in range(4)]
            with contextlib.ExitStack() as ea:
                WA = [WSH[i - 4] if 4 <= i < 8 else fw.sb(f"WA{i}", [128, 8, 512], BF16, dma=True, es=ea) for i in range(12)]
                WO = [fw.sb(f"WO{i}", [128, 16, 512], BF16, dma=True, es=ea) for i in range(2)]
                wsTm = fw.sb("wsTm", [128, 8, 128], BF16, dma=True, es=ea)
                cmask = fw.sb("cmask", [128, 128], BF16, dma=True, es=ea)
                btile = fw.sb("btile", [128, 8, 128], BF16, es=ea)
                bs0 = fw.sb("bs0", [128, 8], F32, dma=True, es=ea)
                w00 = fw.sb("w00", [128, 8], F32, dma=True, es=ea)
                xs = [fw.sb(f"xs{i}", [128, D], F32, dma=True, es=ea) for i in range(2)]
                hs = [fw.sb(f"hsA{i}", [128, D], F32, dma=True, es=ea) for i in range(2)]
                xsn2 = [fw.sb(f"xsn{i}", [128, D], BF16, es=ea) for i in range(2)]
                xT = fw.sb("xT", [128, 8, 512], BF16, es=ea)
                vraw = [fw.sb("vraw0", [128, AW], BF16, es=ea)]
                wss = [fw.sb("wss0", [128, 8, 128], BF16, es=ea)]
                yT = fw.sb("yT", [128, 16, 512], BF16, es=ea)
                sgb = [fw.sb(f"sgb{i}", [128, 512], BF16, es=ea) for i in range(2)]
                usg = [fw.sb(f"usg{i}", [128, 512], BF16, es=ea) for i in range(2)]
                zzb = [fw.sb(f"zzb{i}", [128, 512], BF16, es=ea) for i in range(2)]
                pa = [fw.ps(f"pa{i}", [128, 512], F32, es=ea) for i in range(8)]
                pav = {id(t_): TV(t_, t_.ap[:].bitcast(BF16).rearrange("p (k t) -> p k t", k=8)) for t_ in pa}
                pac = [0]

                def next_pa():
                    pac[0] += 1
                    return pa[pac[0] % 8]
                eh = ea.enter_context(contextlib.ExitStack())
                vraw += [fw.sb(f"vraw{i}", [128, AW], BF16, es=eh) for i in range(1, 4)]
                wss += [fw.sb(f"wss{i}", [128, 8, 128], BF16, es=eh) for i in range(1, 4)]

                fw.dma(pool, cmask[:], cmask_d[:, :], cmask, writes=[cmask])
                fw.dma(pool, wsTm[:].rearrange("p g i -> p (g i)"), wsT_d[:, :], wsTm, writes=[wsTm])
                fw.dma(sp, xs[0][:], bs_d.partition_broadcast(128), xs[0], writes=[xs[0]])
                fw.op(act, lambda: S.activation(out=btile[:].rearrange("p g i -> p (g i)"), in_=xs[0][:], func=AF.Copy),
                      reads=[xs[0]], writes=[btile])
                fw.dma(sp, bs0[:], bs0_d.partition_broadcast(128), bs0, writes=[bs0])
                fw.dma(sp, w00[:], w00_d.partition_broadcast(128), w00, writes=[w00])
                w_in_v = w_in_a.rearrange("(k p) c -> p k c", p=128)
                for i in [4, 5, 6, 7, 0, 8, 1, 9, 2, 10, 3, 11]:
                    fw.dma(pool, WA[i][:], w_in_v[:, :, i * 512:(i + 1) * 512], WA[i], writes=[WA[i]])
                w_out_v = w_out_a.rearrange("(k p) c -> p k c", p=128)
                for i in range(2):
                    fw.dma(pool, WO[i][:], w_out_v[:, :, i * 512:(i + 1) * 512], WO[i], writes=[WO[i]])
                fw.op(dve, lambda: V.tensor_tensor(out=wsTm[:], in0=wsTm[:],
                                                   in1=cmask[:].unsqueeze(1).broadcast_to([128, 8, 128]), op=ALU.mult),
                      reads=[wsTm, cmask], writes=[wsTm])
                ckpt('consts')
                wd = None

                def front_stats(x_rows, c, nt, sample):
                    xb = xs[c % 2]
                    xsn = xsn2[c % 2]
                    nld = NS if sample else nt
                    fw.dma(sp, xb[0:nld, :], x_rows(c), xb, writes=[xb])
                    fw.op(act, lambda: S.activation(out=xsn[0:nt, :], in_=xb[0:nt, :], func=AF.Square,
                                                    accum_out=ssx[0:nt, :]), reads=[xb], writes=[xsn, ssx])
                    rstd_from_ss(ssx[0:nt, 0:1], tmx, rsx, nt, D)
                    fw.op(act, lambda: S.activation(out=xsn[0:nt, :], in_=xb[0:nt, :], func=AF.Copy,
                                                    scale=rsx[0:nt, 0:1]), reads=[xb, rsx], writes=[xsn])

                xTc = [T(f"xTc{c}", xT.ap[:, :, c * 128:(c + 1) * 128]) for c in range(4)]

                def front_T(c, nt):
                    xsn = xsn2[c % 2]
                    pT = pav[id(next_pa())]
                    fw.group([lambda k=k: P_.transpose(pT[:, k, 0:nt], xsn[0:nt, k * 128:(k + 1) * 128],
                                                       identb[0:nt, 0:nt]) for k in range(8)],
                             reads=[xsn, identb], writes=[pT])
                    fw.op(dve, lambda: V.tensor_tensor(out=xTc[c][:, :, 0:nt], in0=pT[:, :, 0:nt],
                                                       in1=gaT[:].unsqueeze(2).broadcast_to([128, 8, nt]), op=ALU.mult),
                          reads=[pT, gaT], writes=[xTc[c]])

                def block_prologue(x_rows, nch, nt, sample):
                    front_stats(x_rows, 0, nt, sample)
                    if nch > 1:
                        front_stats(x_rows, 1, nt, sample)
                    front_T(0, nt)

                def layer_a_block(x_rows, h_rows, nch, nt, sample, pre_done=False, next_front=None):
                    N = (nch - 1) * 128 + nt
                    if not pre_done:
                        block_prologue(x_rows, nch, nt, sample)
                    for c in range(nch):
                        if c + 2 < nch:
                            front_stats(x_rows, c + 2, nt, sample)
                        if c + 1 < nch:
                            front_T(c + 1, nt)
                        ckpt('s_xT' if sample else 'xT')
                        for cb in range(4):
                            pvb = next_pa()
                            wt = WA[4 + cb]
                            fw.group([lambda k=k: P_.matmul(pvb[0:nt, :], xTc[c][:, k, 0:nt], wt[:, k, :],
                                                            start=(k == 0), stop=(k == 7)) for k in range(8)],
                                     reads=[xTc[c], wt], writes=[pvb])
                            fw.op(act, lambda: S.activation(out=vraw[c][0:nt, cb * 512:(cb + 1) * 512], in_=pvb[0:nt, :],
                                                            func=AF.Square, accum_out=ssv[0:nt, cb:cb + 1]),
                                  reads=[pvb], writes=[vraw[c], ssv])
                            fw.op(act, lambda: S.activation(out=vraw[c][0:nt, cb * 512:(cb + 1) * 512], in_=pvb[0:nt, :],
                                                            func=AF.Copy), reads=[pvb], writes=[vraw[c]])
                            if sample and 'vf32' not in KSKIP:
                                fw.op(act, lambda: S.activation(out=vf32[0:nt, cb * 512:(cb + 1) * 512], in_=pvb[0:nt, :],
                                                                func=AF.Copy), reads=[pvb], writes=[vf32])
                        fw.op(dve, lambda: V.reduce_sum(out=ssv1[0:nt, :], in_=ssv[0:nt, :], axis=AX.X),
                              reads=[ssv], writes=[ssv1])
                        rstd_from_ss(ssv1[0:nt, 0:1], tmv, rsv, nt, AW)
                        wsrc = wd if sample else wsTm
                        fw.op(dve, lambda: V.tensor_scalar(out=wss[c][0:nt, :, 0:nt], in0=wsrc[0:nt, :, 0:nt],
                                                           scalar1=rsv[0:nt, 0:1], scalar2=None, op0=ALU.mult),
                              reads=[wsrc, rsv], writes=[wss[c]])
                        ckpt('s_vproj' if sample else 'vproj')
                        if sample and 'nav' not in KSKIP:
                            fw.op(dve, lambda: V.scalar_tensor_tensor(out=vf32[0:nt, :], in0=vf32[0:nt, :], scalar=rsv[0:nt, 0:1],
                                                                      in1=gvrow[0:nt, :], op0=ALU.mult, op1=ALU.mult),
                                  reads=[vf32, rsv, gvrow], writes=[vf32])
                            fw.dma(sp, nav[:, :], vf32[0:NS, :], vf32, reads=[vf32])
                    if sample:
                        fw.dma(pool, WSH[0][:], w_kv.rearrange("(k p) c -> p k c", p=128), WSH[0], writes=[WSH[0]])
                        w_inb_v0 = w_in_b.rearrange("(k p) c -> p k c", p=128)
                        for i_ in range(3):
                            fw.dma(pool, WSH[1 + i_][:], w_inb_v0[:, :, i_ * 512:(i_ + 1) * 512], WSH[1 + i_],
                                   writes=[WSH[1 + i_]])
                    for cc in range(16):
                        g = cc // 2
                        wu = WA[cc // 4]
                        wg = WA[8 + cc // 4]
                        co = (cc % 4) * 128
                        pg, pu, pz = next_pa(), next_pa(), next_pa()
                        fw.group([lambda k=k: P_.matmul(pg[:, 0:N], wg[:, k, co:co + 128], xT[:, k, 0:N],
                                                        start=(k == 0), stop=(k == 7)) for k in range(8)],
                                 reads=[wg] + xTc[:nch], writes=[pg])
                        fw.group([lambda k=k: P_.matmul(pu[:, 0:N], wu[:, k, co:co + 128], xT[:, k, 0:N],
                                                        start=(k == 0), stop=(k == 7)) for k in range(8)],
                                 reads=[wu] + xTc[:nch], writes=[pu])
                        fw.group([lambda c=c: P_.matmul(pz[:, c * 128:c * 128 + nt], vraw[c][0:nt, cc * 128:(cc + 1) * 128],
                                                        wss[c][0:nt, g, 0:nt], start=True, stop=True) for c in range(nch)],
                                 reads=list(vraw[:nch]) + list(wss[:nch]), writes=[pz])
                        sg_, us_, zz_ = sgb[cc % 2], usg[cc % 2], zzb[cc % 2]
                        fw.op(act, lambda: S.activation(out=sg_[:, 0:N], in_=pg[:, 0:N], func=AF.Silu),
                              reads=[pg], writes=[sg_])
                        fw.op(dve, lambda: V.tensor_tensor(out=us_[:, 0:N], in0=pu[:, 0:N], in1=sg_[:, 0:N], op=ALU.mult),
                              reads=[pu, sg_], writes=[us_])
                        if sample:
                            fw.op(dve, lambda: V.scalar_tensor_tensor(out=zz_[:, 0:N], in0=pz[:, 0:N], scalar=gvT[:, cc:cc + 1],
                                                                      in1=bs0[:, g:g + 1].broadcast_to([128, N]),
                                                                      op0=ALU.mult, op1=ALU.add),
                                  reads=[pz, gvT, bs0], writes=[zz_])
                        else:
                            fw.op(dve, lambda: V.scalar_tensor_tensor(
                                out=zz_[:, 0:N].rearrange("p (c i) -> p c i", c=nch),
                                in0=pz[:, 0:N].rearrange("p (c i) -> p c i", c=nch), scalar=gvT[:, cc:cc + 1],
                                in1=btile[:, g, :].unsqueeze(1).broadcast_to([128, nch, 128]),
                                op0=ALU.mult, op1=ALU.add), reads=[pz, gvT, btile], writes=[zz_])
                        fw.op(pool, lambda: G.tensor_tensor(out=yT[:, cc, 0:N], in0=zz_[:, 0:N], in1=us_[:, 0:N], op=ALU.mult),
                              reads=[zz_, us_], writes=[yT])
                        ckpt('s_cc0' if sample else 'cc0')
                    if next_front is not None:
                        next_front()
                    for c in range(nch):
                        hb = hs[c % 2]
                        fw.dma(sp, hb[0:(NS if sample else nt), :], x_rows(c), hb, writes=[hb])
                        for db in range(2):
                            pob = next_pa()
                            fw.group([lambda cc=cc: P_.matmul(pob[0:nt, :], yT[:, cc, c * 128:c * 128 + nt], WO[db][:, cc, :],
                                                              start=(cc == 0), stop=(cc == 15)) for cc in range(16)],
                                     reads=[yT, WO[db]], writes=[pob])
                            fw.op(dve, lambda: V.tensor_tensor(out=hb[0:nt, db * 512:(db + 1) * 512], in0=pob[0:nt, :],
                                                               in1=hb[0:nt, db * 512:(db + 1) * 512], op=ALU.add),
                                  reads=[pob, hb], writes=[hb])
                        fw.dma(pool, h_rows(c), hb[0:nt, :], hb, reads=[hb])
                        ckpt('s_chunk0' if sample else 'chunk0')

                def xrows(blk):
                    return lambda c: xp[(blk * 4 + c) * 128:(blk * 4 + c + 1) * 128, :]

                for blk in range(4):
                    nf = (lambda blk=blk: block_prologue(xrows(blk + 1), 4, 128, False)) if blk < 3 else None
                    layer_a_block(xrows(blk), lambda c, blk=blk: h1s[(blk * 4 + c) * 128:(blk * 4 + c + 1) * 128, :],
                                  4, 128, False, pre_done=(blk > 0), next_front=nf)
                ckpt('blockA')
                fw.barrier_all()
                eh.close()
                vf32 = fw.sb("vf32", [128, AW], F32, dma=True, es=ea)
                gvrow = fw.sb("gvrow", [128, AW], F32, dma=True, es=ea)
                wd = fw.sb("wd", [128, 8, 128], BF16, es=ea)
                fw.dma(sp, gvrow[:], gv_row.partition_broadcast(128), gvrow, writes=[gvrow])
                fw.op(dve, lambda: V.tensor_tensor(out=wd[:], in0=identb[:].unsqueeze(1).broadcast_to([128, 8, 128]),
                                                   in1=w00[:].unsqueeze(2).broadcast_to([128, 8, 128]), op=ALU.mult),
                      reads=[identb, w00], writes=[wd])
                for tz in (xs[0], hs[0]):
                    fw.op(pool, lambda: G.memset(tz[:], 0.0), writes=[tz])
                layer_a_block(lambda c: xsm[:, :], lambda c: h1s[SEQ:SEQ + 128, :], 1, 128, True)
                ckpt('sampleA')
                fw.barrier_all()

            with contextlib.ExitStack() as eb:
                WKV = WSH[0]
                WQG = [WSH[1], WSH[2], WSH[3], fw.sb("WQG3", [128, 8, 512], BF16, dma=True, es=eb)]
                WOB = [fw.sb(f"WOB{i}", [128, 8, 512], BF16, dma=True, es=eb) for i in range(2)]
                w_inb_v = w_in_b.rearrange("(k p) c -> p k c", p=128)
                fw.dma(pool, WQG[3][:], w_inb_v[:, :, 3 * 512:4 * 512], WQG[3], writes=[WQG[3]])
                w_outb_v = w_out_b.rearrange("(k p) c -> p k c", p=128)
                for i in range(2):
                    fw.dma(pool, WOB[i][:], w_outb_v[:, :, i * 512:(i + 1) * 512], WOB[i], writes=[WOB[i]])
                mprev = fw.sb("mprev", [128, 512], BF16, dma=True, es=eb)
                mcur = fw.sb("mcur", [128, 512], BF16, dma=True, es=eb)
                fw.dma(pool, mprev[:], mprev_d[:, :], mprev, writes=[mprev])
                fw.dma(pool, mcur[:], mcur_d[:, :], mcur, writes=[mcur])
                gfrow = fw.sb("gfrow", [128, D], F32, dma=True, es=eb)
                fw.dma(sp, gfrow[:], gf_row.partition_broadcast(128), gfrow, writes=[gfrow])
                esink = fw.sb("esink", [128, 16], F32, dma=True, es=eb)
                fw.dma(sp, esink[:], sinks_d.partition_broadcast(128), esink, writes=[esink])
                fw.op(act, lambda: S.activation(out=esink[:], in_=esink[:], func=AF.Exp), reads=[esink], writes=[esink])
                onesb = fw.sb("onesb", [128, 128], BF16, es=eb)
                fw.op(pool, lambda: G.memset(onesb[:], 1.0), writes=[onesb])

                hs = [fw.sb(f"hsB{i}", [128, D], F32, dma=True, es=eb) for i in range(4)]
                ys = [fw.sb(f"ysB{i}", [128, D], F32, dma=True, es=eb) for i in range(2)]
                rope = [fw.sb(f"rope{i}", [128, 16], F32, dma=True, es=eb) for i in range(2)]
                hsn = fw.sb("hsn", [128, D], BF16, es=eb)
                hkT = fw.sb("hkT", [128, 8, 128], BF16, es=eb)
                hbT = fw.sb("hbT", [128, 8, 128], BF16, es=eb)
                kr = [fw.sb(f"kr{i}", [128, 4, 64], F32, dma=True, es=eb) for i in range(2)]
                vf = [fw.sb(f"vf{i}", [128, 4, 64], F32, dma=True, es=eb) for i in range(2)]
                ta = fw.sb("ta", [128, 16, 8], F32, es=eb)
                tb = fw.sb("tb", [128, 16, 8], F32, es=eb)
                tc = fw.sb("tc", [128, 16, 8], F32, es=eb)
                td = fw.sb("td", [128, 16, 8], F32, es=eb)
                xf16 = fw.sb("xf16", [128, 8, 16], F32, es=eb)
                egb = [fw.sb(f"egb{i}", [128, 512], F32, es=eb) for i in range(2)]
                xq16 = [fw.sb(f"xq16_{i}", [128, 8, 16], F32, es=eb) for i in range(2)]
                kdup = fw.sb("kdup", [128, 4, 2, 64], BF16, es=eb)
                kTz = [fw.sb(f"kTz{i}", [128, 4, 2, 128], BF16, es=eb) for i in range(3)]
                vaug = [fw.sb(f"vaug{i}", [128, 4, 65], BF16, es=eb) for i in range(3)]
                qr = fw.sb("qr", [128, 16, 64], BF16, es=eb)
                sgt2 = [fw.sb(f"sgt{i}", [128, D], BF16, es=eb) for i in range(2)]
                qT2 = [fw.sb(f"qT{i}", [128, 8, 128], BF16, es=eb) for i in range(2)]
                PT = [fw.sb(f"PT{i}", [128, 512], BF16, es=eb) for i in range(4)]
                den = fw.sb("den", [128, 16], F32, es=eb)
                rden = fw.sb("rden", [128, 16], F32, es=eb)
                on = fw.sb("on", [128, 16, 64], BF16, es=eb)
                og = fw.sb("og", [128, D], BF16, es=eb)
                ogT = fw.sb("ogT", [128, 8, 128], BF16, es=eb)
                pp = [fw.ps(f"pp{i}", [128, 512], F32, es=eb) for i in range(5)]
                ppv = {id(t_): TV(t_, t_.ap[:].bitcast(BF16).rearrange("p (k t) -> p k t", k=8)) for t_ in pp}

                def next_pT():
                    return ppv[id(next_pp())]
                po = [fw.ps(f"poB{i}", [128, 7, 72], F32, es=eb) for i in range(3)]
                for i in range(3):
                    fw.op(pool, lambda: G.memset(kTz[i][:], 0.0), writes=[kTz[i]])
                    fw.op(pool, lambda: G.memset(vaug[i][:], 1.0), writes=[vaug[i]])
                ppc = [0]
                kcA = [fw.sb(f"kcA{i}", [128, 4, 4, 2, 64], BF16, dma=True, es=eb) for i in range(4)]
                vcA = [fw.sb(f"vcA{i}", [128, 4, 4, 2, 64], BF16, dma=True, es=eb) for i in range(4)]
                kTA = [fw.sb(f"kTA{i}", [128, 4, 4, 128], BF16, es=eb) for i in range(4)]
                kct, vct = [], []

                def load_cache_group(i):
                    for (grp, src, lst) in ((kcA[i], ck, kct), (vcA[i], cv, vct)):
                        g4 = []
                        for bb in range(4):
                            tk = T(f"{grp.name}_{bb}", grp.ap[:, bb])
                            tk.dsem_sw = fw.holder(grp, pool)
                            fw.dma(pool, tk[:, :, 0, :], src[4 * i + bb].rearrange("s (h d) -> s h d", h=4), tk, writes=[tk])
                            g4.append(tk)
                        for tk in g4:
                            tk.w = (grp.dsem_sw, grp.dsem_sw.cnt)
                        lst.extend(g4)
                ckpt('b_setup')

                def next_pp():
                    ppc[0] += 1
                    return pp[ppc[0] % 5]

                def rotary(xf, nh, dst, nt, rp, writes_t):
                    x1 = xf[0:nt, 0:nh, 0:8]
                    x2 = xf[0:nt, 0:nh, 8:16]
                    cos = rp[0:nt, 0:8].unsqueeze(1).broadcast_to([nt, nh, 8])
                    sin = rp[0:nt, 8:16].unsqueeze(1).broadcast_to([nt, nh, 8])
                    fw.op(dve, lambda: V.tensor_tensor(out=ta[0:nt, 0:nh, :], in0=x1, in1=cos, op=ALU.mult),
                          reads=[xf, rp], writes=[ta])
                    fw.op(dve, lambda: V.tensor_tensor(out=tb[0:nt, 0:nh, :], in0=x2, in1=sin, op=ALU.mult),
                          reads=[xf, rp], writes=[tb])
                    fw.op(dve, lambda: V.tensor_tensor(out=tc[0:nt, 0:nh, :], in0=x2, in1=cos, op=ALU.mult),
                          reads=[xf, rp], writes=[tc])
                    fw.op(dve, lambda: V.tensor_tensor(out=td[0:nt, 0:nh, :], in0=x1, in1=sin, op=ALU.mult),
                          reads=[xf, rp], writes=[td])
                    fw.op(dve, lambda: V.tensor_tensor(out=dst[:, :, 0:8], in0=ta[0:nt, 0:nh, :], in1=tb[0:nt, 0:nh, :],
                                                       op=ALU.subtract), reads=[ta, tb], writes=[writes_t])
                    fw.op(dve, lambda: V.tensor_tensor(out=dst[:, :, 8:16], in0=tc[0:nt, 0:nh, :], in1=td[0:nt, 0:nh, :],
                                                       op=ALU.add), reads=[tc, td], writes=[writes_t])

                def stage_a_load(ti, nt, rows_in, rope_rows):
                    hb = hs[ti % 4]
                    rp = rope[ti % 2]
                    fw.dma(sp, hb[0:nt, :], rows_in, hb, writes=[hb])
                    fw.dma(sp, rp[0:nt, :], rope_rows, rp, writes=[rp])

                def stage_a1(ti, nt):
                    hb = hs[ti % 4]
                    fw.op(act, lambda: S.activation(out=hsn[0:nt, :], in_=hb[0:nt, :], func=AF.Square,
                                                    accum_out=ssx[0:nt, :]), reads=[hb], writes=[hsn, ssx])
                    fw.op(dve, lambda: V.tensor_scalar(out=tmx[0:nt, 0:1], in0=ssx[0:nt, 0:1], scalar1=1.0 / D, scalar2=EPS,
                                                       op0=ALU.mult, op1=ALU.add), reads=[ssx], writes=[tmx])

                def stage_a2(ti, nt):
                    hb = hs[ti % 4]
                    fw.op(act, lambda: S.activation(out=tmx[0:nt, 0:1], in_=tmx[0:nt, 0:1], func=AF.Ln),
                          reads=[tmx], writes=[tmx])
                    fw.op(act, lambda: S.activation(out=rsx[0:nt, 0:1], in_=tmx[0:nt, 0:1], func=AF.Exp, scale=-0.5),
                          reads=[tmx], writes=[rsx])
                    fw.op(act, lambda: S.activation(out=hsn[0:nt, :], in_=hb[0:nt, :], func=AF.Copy,
                                                    scale=rsx[0:nt, 0:1]), reads=[hb, rsx], writes=[hsn])

                def stage_a(ti, nt):
                    stage_a1(ti, nt)
                    stage_a2(ti, nt)

                def stage_b1(ti, nt):
                    pT = next_pT()
                    fw.group([lambda k=k: P_.transpose(pT[:, k, 0:nt], hsn[0:nt, k * 128:(k + 1) * 128],
                                                       identb[0:nt, 0:nt]) for k in range(8)],
                             reads=[hsn, identb], writes=[pT])
                    fw.op(dve, lambda: V.tensor_tensor(out=hkT[:, :, 0:nt], in0=pT[:, :, 0:nt],
                                                       in1=gkvT[:].unsqueeze(2).broadcast_to([128, 8, nt]), op=ALU.mult),
                          reads=[pT, gkvT], writes=[hkT])
                    fw.op(dve, lambda: V.tensor_tensor(out=hbT[:, :, 0:nt], in0=pT[:, :, 0:nt],
                                                       in1=gbT[:].unsqueeze(2).broadcast_to([128, 8, nt]), op=ALU.mult),
                          reads=[pT, gbT], writes=[hbT])

                def stage_b2(ti, nt, par, sample):
                    sgt = sgt2[ti % 2]
                    par3 = ti % 3
                    rp = rope[ti % 2]
                    pk = next_pp()
                    fw.group([lambda k=k: P_.matmul(pk[0:nt, :], hkT[:, k, 0:nt], WKV[:, k, :], start=(k == 0), stop=(k == 7))
                              for k in range(8)], reads=[hkT, WKV], writes=[pk])
                    pqs = []
                    for cb in range(2 if sample else 4):
                        pq = next_pp() if cb < 2 else None
                        pqs.append(pq)
                    krb, vfb = kr[par], vf[par]
                    fw.op(act, lambda: S.activation(out=krb[0:nt].rearrange("p h d -> p (h d)"), in_=pk[0:nt, 0:256],
                                                    func=AF.Copy), reads=[pk], writes=[krb])
                    fw.op(act, lambda: S.activation(out=xf16[0:nt, 0:4, :],
                                                    in_=pk[0:nt, 0:256].rearrange("p (h d) -> p h d", h=4)[:, :, 0:16],
                                                    func=AF.Copy), reads=[pk], writes=[xf16])
                    fw.op(act, lambda: S.activation(out=vfb[0:nt].rearrange("p h d -> p (h d)"), in_=pk[0:nt, 256:512],
                                                    func=AF.Copy), reads=[pk], writes=[vfb])
                    fw.op(act, lambda: S.activation(out=vaug[par3][0:nt, :, 0:64],
                                                    in_=pk[0:nt, 256:512].rearrange("p (h d) -> p h d", h=4), func=AF.Copy),
                          reads=[pk], writes=[vaug[par3]])
                    rotary(xf16, 4, krb[0:nt], nt, rp, krb)
                    fw.op(pool, lambda: G.tensor_copy(out=kdup[0:nt], in_=krb[0:nt].unsqueeze(2).broadcast_to([nt, 4, 2, 64])),
                          reads=[krb], writes=[kdup])
                    for cb in range(2):
                        pq = pqs[cb]
                        fw.group([lambda k=k: P_.matmul(pq[0:nt, :], hbT[:, k, 0:nt], WQG[cb][:, k, :],
                                                        start=(k == 0), stop=(k == 7)) for k in range(8)],
                                 reads=[hbT, WQG[cb]], writes=[pq])
                        fw.op(act, lambda: S.activation(out=qr[0:nt, cb * 8:(cb + 1) * 8].rearrange("p h d -> p (h d)"),
                                                        in_=pq[0:nt, :], func=AF.Copy), reads=[pq], writes=[qr])
                        xq = xq16[cb]
                        fw.op(act, lambda: S.activation(out=xq[0:nt, :, :],
                                                        in_=pq[0:nt, :].rearrange("p (h d) -> p h d", h=8)[:, :, 0:16],
                                                        func=AF.Copy), reads=[pq], writes=[xq])
                        rotary(xq, 8, qr[0:nt, cb * 8:(cb + 1) * 8], nt, rp, qr)
                    if not sample:
                        for cb in range(2):
                            pq = next_pp()
                            fw.group([lambda k=k: P_.matmul(pq[0:nt, :], hbT[:, k, 0:nt], WQG[2 + cb][:, k, :],
                                                            start=(k == 0), stop=(k == 7)) for k in range(8)],
                                     reads=[hbT, WQG[2 + cb]], writes=[pq])
                            fw.op(act, lambda: S.activation(out=sgt[0:nt, cb * 512:(cb + 1) * 512], in_=pq[0:nt, :],
                                                            func=AF.Silu), reads=[pq], writes=[sgt])

                def stage_b3(ti, nt):
                    qT = qT2[ti % 2]
                    par3 = ti % 3
                    pT = next_pT()
                    fw.group([lambda kh=kh: P_.transpose(pT[:, kh, 0:nt], kdup[0:nt, kh].rearrange("p r d -> p (r d)"),
                                                         identb[0:nt, 0:nt]) for kh in range(4)],
                             reads=[kdup, identb], writes=[pT])
                    fw.op(act, lambda: S.activation(out=kTz[par3][0:64, :, 0, 0:nt], in_=pT[0:64, 0:4, 0:nt], func=AF.Copy),
                          reads=[pT], writes=[kTz[par3]])
                    fw.op(act, lambda: S.activation(out=kTz[par3][64:128, :, 1, 0:nt], in_=pT[64:128, 0:4, 0:nt], func=AF.Copy),
                          reads=[pT], writes=[kTz[par3]])
                    pT = next_pT()
                    fw.group([lambda m=m: P_.transpose(pT[:, m, 0:nt], qr[0:nt, 2 * m:2 * m + 2].rearrange("p h d -> p (h d)"),
                                                       identb[0:nt, 0:nt]) for m in range(8)],
                             reads=[qr, identb], writes=[pT])
                    fw.op(dve, lambda: V.tensor_tensor(out=qT[:, :, 0:nt], in0=pT[:, :, 0:nt],
                                                       in1=onesf[:].unsqueeze(2).broadcast_to([128, 8, nt]), op=ALU.mult),
                          reads=[pT, onesf], writes=[qT])

                def layer_b_tail(ti, nt, hb, rows_out, n_out=128):
                    for db in range(2):
                        pob = next_pp()
                        fw.group([lambda m=m: P_.matmul(pob[0:nt, :], ogT[:, m, 0:nt], WOB[db][:, m, :],
                                                        start=(m == 0), stop=(m == 7)) for m in range(8)],
                                 reads=[ogT, WOB[db]], writes=[pob])
                        fw.op(dve, lambda: V.tensor_tensor(out=hb[0:nt, db * 512:(db + 1) * 512], in0=pob[0:nt, :],
                                                           in1=hb[0:nt, db * 512:(db + 1) * 512], op=ALU.add),
                              reads=[pob, hb], writes=[hb])
                    yb = ys[ti % 2]
                    fw.op(act, lambda: S.activation(out=yb[0:nt, :], in_=hb[0:nt, :], func=AF.Square,
                                                    accum_out=ssv1[0:nt, :]), reads=[hb], writes=[yb, ssv1])
                    rstd_from_ss(ssv1[0:nt, 0:1], tmv, rsv, nt, D)
                    fw.op(dve, lambda: V.scalar_tensor_tensor(out=yb[0:nt, :], in0=hb[0:nt, :], scalar=rsv[0:nt, 0:1],
                                                              in1=gfrow[0:nt, :], op0=ALU.mult, op1=ALU.mult),
                          reads=[hb, rsv, gfrow], writes=[yb])
                    fw.dma(pool, rows_out, yb[0:n_out, :], yb, reads=[yb])

                def attn_scores(ti, kh):
                    qT = qT2[ti % 2]
                    kbs = ([((ti - 1) % 3, mprev)] if ti > 0 else []) + [(ti % 3, mcur)]
                    pts = []
                    for kbi, (kpar, msk) in enumerate(kbs):
                        psb = next_pp()
                        fw.group([
                            lambda: P_.matmul(psb[:, :], identb[:], msk[:], start=True, stop=False),
                            lambda: P_.matmul(psb[:, 0:256], kTz[kpar][:, kh, 0, :],
                                              qT[:, 2 * kh:2 * kh + 2, :].rearrange("p m q -> p (m q)"),
                                              start=False, stop=False),
                            lambda: P_.matmul(psb[:, 256:512], kTz[kpar][:, kh, 1, :],
                                              qT[:, 2 * kh:2 * kh + 2, :].rearrange("p m q -> p (m q)"),
                                              start=False, stop=True),
                        ], reads=[identb, msk, kTz[kpar], qT], writes=[psb])
                        ptb = PT[(kh % 2) * 2 + kbi]
                        fw.op(act, lambda: S.activation(out=ptb[:], in_=psb[:], func=AF.Exp, scale=0.125),
                              reads=[psb], writes=[ptb])
                        pts.append((ptb, kpar))
                    return pts

                def attn_pv(ti, kh, pts):
                    for r in range(2):
                        for mm in range(2):
                            h = 4 * kh + 2 * mm + r
                            bank, slot = po[h // 7], h % 7
                            c0 = r * 256 + mm * 128
                            fw.group([lambda i=i: P_.matmul(bank[:, slot, 0:65], pts[i][0][:, c0:c0 + 128],
                                                            vaug[pts[i][1]][:, kh, :], start=(i == 0),
                                                            stop=(i == len(pts) - 1)) for i in range(len(pts))],
                                     reads=[p[0] for p in pts] + [vaug[p[1]] for p in pts], writes=[bank])

                def attn_finish(ti):
                    sgt = sgt2[ti % 2]
                    for b in range(3):
                        h0 = 7 * b
                        nh = min(7, 16 - h0)
                        fw.op(dve, lambda: V.tensor_tensor(out=den[:, h0:h0 + nh].unsqueeze(2), in0=po[b][:, 0:nh, 64:65],
                                                           in1=esink[:, h0:h0 + nh].unsqueeze(2), op=ALU.add),
                              reads=[po[b], esink], writes=[den])
                    fw.op(dve, lambda: V.reciprocal(out=rden[:], in_=den[:]), reads=[den], writes=[rden])
                    for b in range(3):
                        h0 = 7 * b
                        nh = min(7, 16 - h0)
                        fw.op(dve, lambda: V.tensor_tensor(out=on[:, h0:h0 + nh, :], in0=po[b][:, 0:nh, 0:64],
                                                           in1=rden[:, h0:h0 + nh].unsqueeze(2).broadcast_to([128, nh, 64]),
                                                           op=ALU.mult), reads=[po[b], rden], writes=[on])
                    fw.op(pool, lambda: G.tensor_tensor(out=og[:], in0=on[:].rearrange("p h d -> p (h d)"), in1=sgt[:],
                                                        op=ALU.mult), reads=[on, sgt], writes=[og])
                    pT = next_pT()
                    fw.group([lambda m=m: P_.transpose(pT[:, m, :], og[:, m * 128:(m + 1) * 128], identb[:]) for m in range(8)],
                             reads=[og, identb], writes=[pT])
                    fw.op(act, lambda: S.activation(out=ogT[:], in_=pT[:], func=AF.Copy), reads=[pT], writes=[ogT])

                def rows(ti):
                    if ti == NT:
                        return h1s[SEQ:SEQ + 128, :], rope_d[SEQ:SEQ + 128, :]
                    return h1s[ti * 128:(ti + 1) * 128, :], rope_d[ti * 128:(ti + 1) * 128, :]

                def attn_og(ti):
                    sgt = sgt2[ti % 2]
                    for b in range(3):
                        h0 = 7 * b
                        nh = min(7, 16 - h0)
                        fw.op(dve, lambda: V.tensor_tensor(out=den[:, h0:h0 + nh].unsqueeze(2), in0=po[b][:, 0:nh, 64:65],
                                                           in1=esink[:, h0:h0 + nh].unsqueeze(2), op=ALU.add),
                              reads=[po[b], esink], writes=[den])
                    fw.op(dve, lambda: V.reciprocal(out=rden[:], in_=den[:]), reads=[den], writes=[rden])
                    for b in range(3):
                        h0 = 7 * b
                        nh = min(7, 16 - h0)
                        fw.op(dve, lambda: V.tensor_tensor(out=on[:, h0:h0 + nh, :], in0=po[b][:, 0:nh, 0:64],
                                                           in1=rden[:, h0:h0 + nh].unsqueeze(2).broadcast_to([128, nh, 64]),
                                                           op=ALU.mult), reads=[po[b], rden], writes=[on])
                    fw.op(pool, lambda: G.tensor_tensor(out=og[:], in0=on[:].rearrange("p h d -> p (h d)"), in1=sgt[:],
                                                        op=ALU.mult), reads=[on, sgt], writes=[og])

                def og_transpose():
                    pT = next_pT()
                    fw.group([lambda m=m: P_.transpose(pT[:, m, :], og[:, m * 128:(m + 1) * 128], identb[:]) for m in range(8)],
                             reads=[og, identb], writes=[pT])
                    fw.op(act, lambda: S.activation(out=ogT[:], in_=pT[:], func=AF.Copy), reads=[pT], writes=[ogT])

                stage_a_load(0, 128, *rows(0))
                stage_a(0, 128)
                for i in range(-1, NT + 1):
                    nx = i + 1
                    have_nx = nx <= NT
                    cur = 0 <= i < NT
                    if i + 2 <= NT:
                        stage_a_load(i + 2, 128, *rows(i + 2))
                    if 0 <= i < 4:
                        load_cache_group(i)
                    if 6 <= i < 14:
                        gi = (i - 6) // 2
                        ca = (kcA if i % 2 == 0 else vcA)[gi]
                        cts = (kct if i % 2 == 0 else vct)[4 * gi:4 * gi + 4]
                        fw.op(pool, lambda: G.tensor_copy(out=ca[:, :, :, 1, :], in_=ca[:, :, :, 0, :]), reads=cts, writes=cts)
                    if i in (7, 9, 11, 13):
                        gi = (i - 7) // 2
                        for bb in range(4):
                            kcb = kct[4 * gi + bb]
                            pT = next_pT()
                            fw.group([lambda kh=kh: P_.transpose(pT[:, kh, :], kcb[:, kh].rearrange("p r d -> p (r d)"),
                                                                 identb[:]) for kh in range(4)], reads=[kcb, identb], writes=[pT])
                            if bb % 2 == 0:
                                fw.op(act, lambda: S.activation(out=kTA[gi][:, bb], in_=pT[:, 0:4, :], func=AF.Copy),
                                      reads=[pT], writes=[kTA[gi]])
                            else:
                                fw.op(dve, lambda: V.tensor_tensor(out=kTA[gi][:, bb], in0=pT[:, 0:4, :],
                                                                   in1=onesf[:, 0:4].unsqueeze(2).broadcast_to([128, 4, 128]),
                                                                   op=ALU.mult), reads=[pT, onesf], writes=[kTA[gi]])
                    if have_nx:
                        stage_b1(nx, 128)
                    if cur:
                        p0 = attn_scores(i, 0)
                        p1 = attn_scores(i, 1)
                    if have_nx:
                        stage_b2(nx, 128, nx % 2, nx == NT)
                        if nx == NT - 1:
                            kp = nx % 2
                            fw.dma(sp, nkp[:, :], kr[kp][:].rearrange("p h d -> p (h d)"), kr[kp], reads=[kr[kp]])
                            fw.dma(sp, nvp[:, :], vf[kp][:].rearrange("p h d -> p (h d)"), vf[kp], reads=[vf[kp]])
                    if nx + 1 <= NT:
                        stage_a1(nx + 1, 128)
                    if i >= 1:
                        og_transpose()
                    if cur:
                        attn_pv(i, 0, p0)
                        p2 = attn_scores(i, 2)
                        attn_pv(i, 1, p1)
                        p3 = attn_scores(i, 3)
                    if nx + 1 <= NT:
                        stage_a2(nx + 1, 128)
                    if i >= 1:
                        layer_b_tail(i - 1, 128, hs[(i - 1) % 4], yp[(i - 1) * 128:i * 128, :])
                    if cur:
                        attn_pv(i, 2, p2)
                        attn_pv(i, 3, p3)
                    if have_nx:
                        stage_b3(nx, 128)
                    if cur:
                        attn_og(i)

                ckpt('b_prompt')
                with contextlib.ExitStack() as e2:
                    PTs = fw.sb("PTs", [128, NS, 16], BF16, es=e2)
                    sgT = fw.sb("sgT", [128, 8, NS], BF16, es=e2)
                    prod = fw.sb("prod", [128, 16, 64], F32, es=e2)
                    snew = fw.sb("snew", [128, 16], F32, es=e2)
                    pm = fw.sb("pm", [128, NS, 16], BF16, es=e2)
                    vnd = fw.sb("vnd", [128, 4, 2, 64], BF16, es=e2)
                    onew = fw.sb("onew", [128, NS, 16], F32, es=e2)
                    dens = fw.sb("dens", [128, NS, 16], F32, es=e2)
                    osb = fw.sb("osb", [128, NS, 16], F32, es=e2)
                    cpy = T("cpy")
                    fw.add_dsem(cpy)
                    par = 0
                    hb = hs[NT % 4]
                    qT = qT2[NT % 2]
                    krb, vfb = kr[par], vf[par]
                    fw.dma(sp, nks[:, 0:127, :], ck[:, 1:128, :], cpy)
                    fw.dma(sp, nvs[:, 0:127, :], cv[:, 1:128, :], cpy)
                    fw.dma(sp, nks[:, 127, :], krb[0:NS].rearrange("p h d -> p (h d)"), kr[par], reads=[krb])
                    fw.dma(sp, nvs[:, 127, :], vfb[0:NS].rearrange("p h d -> p (h d)"), vf[par], reads=[vfb])
                    pgs = next_pp()
                    for m in range(8):
                        wq = WQG[2 + m // 4]
                        co = (m % 4) * 128
                        fw.group([lambda k=k: P_.matmul(pgs[:, m * NS:(m + 1) * NS], wq[:, k, co:co + 128], hbT[:, k, 0:NS],
                                                        start=(k == 0), stop=(k == 7)) for k in range(8)],
                                 reads=[wq, hbT], writes=[pgs])
                    fw.op(act, lambda: S.activation(out=sgT[:].rearrange("p m b -> p (m b)"), in_=pgs[:, 0:8 * NS],
                                                    func=AF.Silu), reads=[pgs], writes=[sgT])
                    ckpt('s_front')
                    qz = fw.sb("qz", [128, 2, 8, NS], BF16, es=e2)
                    fw.op(pool, lambda: G.memset(qz[:], 0.0), writes=[qz])
                    for r_ in range(2):
                        rw = slice(64 * r_, 64 * r_ + 64)
                        fw.op(dve, lambda: V.tensor_copy(out=qz[rw, r_, :, :], in_=qT[rw, :, 0:NS]), reads=[qT], writes=[qz])
                    pSs = next_pp()
                    for b in range(NS):
                        kta = kTA[b // 4]
                        fns = []
                        for kh in range(4):
                            for r in range(2):
                                fns.append(lambda kh=kh, r=r: P_.matmul(
                                    pSs[:, b * 16 + 4 * kh + r:b * 16 + 4 * kh + r + 3:2], kta[:, b % 4, kh, :],
                                    qz[:, r, 2 * kh:2 * kh + 2, b], start=True, stop=True))
                        fw.group(fns, reads=[kta, qz], writes=[pSs])
                    fw.op(act, lambda: S.activation(out=PTs[:].rearrange("p b h -> p (b h)"), in_=pSs[:, 0:NS * 16],
                                                    func=AF.Exp, scale=0.125), reads=[pSs], writes=[PTs])
                    ckpt('s_scores')
                    pO = next_pp()
                    for b in range(NS):
                        vcb = vct[b]
                        fw.group([lambda kh=kh: P_.matmul(pO[:, b * 16 + 4 * kh:b * 16 + 4 * kh + 4],
                                                          vcb[:, kh].rearrange("p r d -> p (r d)"),
                                                          PTs[:, b, 4 * kh:4 * kh + 4], start=True, stop=True)
                                  for kh in range(4)], reads=[vcb, PTs], writes=[pO])
                    pD = next_pp()
                    fw.group([lambda: P_.matmul(pD[:, 0:NS * 16], onesb[:], PTs[:].rearrange("p b h -> p (b h)"),
                                                start=True, stop=True)], reads=[onesb, PTs], writes=[pD])
                    fw.op(act, lambda: S.activation(out=osb[:].rearrange("p b h -> p (b h)"), in_=pO[:, 0:NS * 16],
                                                    func=AF.Copy), reads=[pO], writes=[osb])
                    fw.op(act, lambda: S.activation(out=dens[:].rearrange("p b h -> p (b h)"), in_=pD[:, 0:NS * 16],
                                                    func=AF.Copy), reads=[pD], writes=[dens])
                    ckpt('s_pv')
                    fw.op(dve, lambda: V.tensor_tensor(out=prod[:].rearrange("p (k g) d -> p k g d", k=4),
                                                       in0=qr[:].rearrange("p (k g) d -> p k g d", k=4),
                                                       in1=krb[:].unsqueeze(2).broadcast_to([128, 4, 4, 64]), op=ALU.mult),
                          reads=[qr, krb], writes=[prod])
                    fw.op(dve, lambda: V.reduce_sum(out=snew[:], in_=prod[:], axis=AX.X), reads=[prod], writes=[snew])
                    fw.op(act, lambda: S.activation(out=snew[:], in_=snew[:], func=AF.Exp, scale=0.125),
                          reads=[snew], writes=[snew])
                    pm4 = pm[:].rearrange("p b h -> p (b h)").rearrange("p (k b g) -> p k b g", k=4, b=NS)
                    fw.op(dve, lambda: V.tensor_tensor(
                        out=pm4, in0=identb[:, 0:NS].unsqueeze(1).unsqueeze(3).broadcast_to([128, 4, NS, 4]),
                        in1=snew[:].rearrange("p (k g) -> p k g", k=4).unsqueeze(2).broadcast_to([128, 4, NS, 4]),
                        op=ALU.mult), reads=[identb, snew], writes=[pm])
                    fw.op(dve, lambda: V.tensor_copy(out=vnd[:], in_=vfb[:].unsqueeze(2).broadcast_to([128, 4, 2, 64])),
                          reads=[vfb], writes=[vnd])
                    pN = next_pp()
                    pmf = pm[:].rearrange("p b h -> p (b h)")
                    fw.group([lambda kh=kh: P_.matmul(
                        pN[:, kh * 64:(kh + 1) * 64], vnd[:, kh].rearrange("p r d -> p (r d)"),
                        pmf[:, kh * 64:(kh + 1) * 64], start=True, stop=True)
                        for kh in range(4)], reads=[vnd, pm], writes=[pN])
                    osb4 = osb[:].rearrange("p b (k g) -> p b k g", k=4)
                    fw.op(dve, lambda: V.tensor_tensor(
                        out=osb4, in0=pN[:, 0:NS * 16].rearrange("p (k b g) -> p b k g", k=4, b=NS), in1=osb4,
                        op=ALU.add), reads=[pN, osb], writes=[osb])
                    pN2 = next_pp()
                    fw.group([lambda: P_.matmul(pN2[:, 0:NS * 16], onesb[:], pm[:].rearrange("p b h -> p (b h)"),
                                                start=True, stop=True)], reads=[onesb, pm], writes=[pN2])
                    dens4 = dens[:].rearrange("p b (k g) -> p b k g", k=4)
                    fw.op(dve, lambda: V.tensor_tensor(
                        out=dens4, in0=pN2[:, 0:NS * 16].rearrange("p (k b g) -> p b k g", k=4, b=NS), in1=dens4,
                        op=ALU.add), reads=[pN2, dens], writes=[dens])
                    fw.op(dve, lambda: V.tensor_tensor(out=dens[:], in0=dens[:],
                                                       in1=esink[:].unsqueeze(1).broadcast_to([128, NS, 16]), op=ALU.add),
                          reads=[dens, esink], writes=[dens])
                    fw.op(dve, lambda: V.reciprocal(out=dens[:], in_=dens[:]), reads=[dens], writes=[dens])
                    fw.op(dve, lambda: V.tensor_tensor(out=osb[:], in0=osb[:], in1=dens[:], op=ALU.mult),
                          reads=[osb, dens], writes=[osb])
                    ckpt('s_new')
                    for r in range(2):
                        rows = slice(64 * r, 64 * r + 64)
                        fw.op(dve, lambda: V.tensor_tensor(
                            out=ogT[rows, :, 0:NS],
                            in0=osb[rows].rearrange("p b (m r) -> p r m b", r=2)[:, r],
                            in1=sgT[rows], op=ALU.mult), reads=[osb, sgT], writes=[ogT])
                    layer_b_tail(NT, 128, hb, ysm[:, :], NS)

        try:
            body()
        except StopBuild:
            pass
        for h in fw.engs + fw.dma_holders:
            if h is not sp and h.cnt > 0 and sp.seen.get(h, 0) < h.cnt:
                sp.h.wait_ge(h.sem, h.cnt)
                sp.seen[h] = h.cnt
    return nc


def _consts():
    pos = np.concatenate([np.arange(SEQ), np.full(128, 8192)]).astype(np.float32)
    inv = (np.float32(500000.0) ** (-np.arange(0, 16, 2, dtype=np.float32) / np.float32(16))).astype(np.float32)
    ang = (pos[:, None] * inv[None, :]).astype(np.float32)
    rope = np.concatenate([np.cos(ang), np.sin(ang)], axis=1).astype(np.float32)
    ident = np.eye(128, dtype=np.float32)
    j = np.arange(128)[:, None]
    i = np.arange(128)[None, :]
    cmask = (j <= i).astype(np.float32)
    mprev = np.where(j >= i, 0.0, NEG).astype(np.float32)
    mcur = np.where(j <= i, 0.0, NEG).astype(np.float32)
    return dict(rope=rope, ident=ident, cmask=cmask, mprev=np.tile(mprev, (1, 4)), mcur=np.tile(mcur, (1, 4)))


_NC_CACHE = {}


def kernel(x_prompt, x_sample, cache_k, cache_v, norm_a, w_in_a, v_norm_a, w_s_a, b_s_a, w_out_a,
           kv_norm, w_kv, norm_b, w_in_b, sinks_b, w_out_b, final_norm):
    f = lambda a: np.ascontiguousarray(np.asarray(a, dtype=np.float32))
    colT = lambda v, k: f(np.asarray(v).reshape(k, 128).T)
    shared = dict(
        w_in_a=f(np.asarray(w_in_a)[0]), w_out_a=f(np.asarray(w_out_a)[0]), w_kv=f(w_kv),
        w_in_b=f(np.asarray(w_in_b)[0]), w_out_b=f(np.asarray(w_out_b)[0]),
        gaT=colT(norm_a, 8), gvT=colT(v_norm_a, 16), gkvT=colT(kv_norm, 8), gbT=colT(norm_b, 8),
        gv_row=f(np.asarray(v_norm_a).reshape(-1)), gf_row=f(np.asarray(final_norm).reshape(-1)),
        wsT=f(np.transpose(np.asarray(w_s_a)[0], (2, 0, 1)).reshape(128, 1024)),
        w00=f(np.asarray(w_s_a)[0, :, 0, 0]), bs=f(np.asarray(b_s_a)[0].reshape(-1)),
        bs0=f(np.asarray(b_s_a)[0, :, 0]), sinks=f(np.asarray(sinks_b).reshape(-1)),
    )
    shared.update(_consts())
    xp = np.asarray(x_prompt, dtype=np.float32)
    xs = np.asarray(x_sample, dtype=np.float32).reshape(128, D)
    ck = np.asarray(cache_k, dtype=np.float32).reshape(128, 128, 256)
    cv = np.asarray(cache_v, dtype=np.float32).reshape(128, 128, 256)
    in_maps = []
    for c in range(NCORES):
        m = dict(shared)
        m["xp"] = f(xp[c])
        m["xsm"] = f(xs[c * NS:(c + 1) * NS])
        m["ck"] = f(ck[c * NS:(c + 1) * NS])
        m["cv"] = f(cv[c * NS:(c + 1) * NS])
        in_maps.append(m)
    if "nc" not in _NC_CACHE:
        _NC_CACHE["nc"] = build_program()
    res = run_bass_kernel_spmd(_NC_CACHE["nc"], in_maps, core_ids=list(range(NCORES)))
    R = res.results
    y_prompt = np.stack([R[c]["yp"] for c in range(NCORES)]).astype(np.float32)
    y_sample = np.concatenate([R[c]["ysm"] for c in range(NCORES)]).reshape(128, 1, D).astype(np.float32)
    nk_p = np.stack([R[c]["nkp"] for c in range(NCORES)]).reshape(8, 128, 4, 64).astype(np.float32)
    nv_p = np.stack([R[c]["nvp"] for c in range(NCORES)]).reshape(8, 128, 4, 64).astype(np.float32)
    nk_s = np.concatenate([R[c]["nks"] for c in range(NCORES)]).reshape(128, 128, 4, 64).astype(np.float32)
    nv_s = np.concatenate([R[c]["nvs"] for c in range(NCORES)]).reshape(128, 128, 4, 64).astype(np.float32)
    nav = np.concatenate([R[c]["nav"] for c in range(NCORES)]).reshape(1, 128, 1, AW).astype(np.float32)
    return (y_prompt, y_sample, nk_p, nv_p, nk_s, nv_s, nav)
```

```python
import contextlib
import numpy as np
import concourse.bass as bass
import concourse.mybir as mybir
from concourse.bass_utils import run_bass_kernel_spmd

F32 = mybir.dt.float32
BF16 = mybir.dt.bfloat16
AF = mybir.ActivationFunctionType
ALU = mybir.AluOpType
AX = mybir.AxisListType

NCORES = 8
D = 1024
SEQ = 2048
NT = SEQ // 128
NS = 16
AW = 2048
EPS = 1e-5
NEG = -30000.0


class Eng:
    def __init__(self, name, h, sem):
        self.name = name
        self.h = h
        self.sem = sem
        self.cnt = 0
        self.seen = {}


class T:
    def __init__(self, name, ap=None):
        self.name = name
        self.ap = ap
        self.w = None
        self.r = {}
        self.dsem = None
        self.dsem_sw = None
        self.psum = False

    def __getitem__(self, k):
        return self.ap[k]


class TV:
    def __init__(self, base, ap):
        object.__setattr__(self, "base", base)
        object.__setattr__(self, "ap", ap)

    def __getattr__(self, k):
        return getattr(object.__getattribute__(self, "base"), k)

    def __setattr__(self, k, v):
        setattr(object.__getattribute__(self, "base"), k, v)

    def __getitem__(self, k):
        return object.__getattribute__(self, "ap")[k]


class FW:
    def __init__(self, nc, es):
        self.nc = nc
        self.es = es
        self.nsem = 0
        mk = lambda n, h: Eng(n, h, self.new_sem(n))
        self.pe = mk("pe", nc.tensor)
        self.act = mk("act", nc.scalar)
        self.dve = mk("dve", nc.vector)
        self.pool = mk("pool", nc.gpsimd)
        self.sp = mk("sp", nc.sync)
        self.engs = [self.pe, self.act, self.dve, self.pool, self.sp]
        self.dma_holders = []
        self.muted = False

    def new_sem(self, name):
        self.nsem += 1
        return self.es.enter_context(self.nc.semaphore(f"s{self.nsem}_{name}"))

    def sb(self, name, shape, dt, dma=False, es=None):
        t = (es or self.es).enter_context(self.nc.sbuf_tensor("sb_" + name, list(shape), dt))
        tt = T(name, t)
        if dma:
            self.add_dsem(tt)
        return tt

    def add_dsem(self, tt):
        pass

    def holder(self, tt, issuer):
        attr = "dsem_sw" if issuer is self.pool else "dsem"
        h = getattr(tt, attr, None)
        if h is None:
            h = Eng(attr + "_" + tt.name, None, self.new_sem("d"))
            setattr(tt, attr, h)
            self.dma_holders.append(h)
        return h

    def ps(self, name, shape, dt, es=None):
        t = (es or self.es).enter_context(self.nc.psum_tensor("ps_" + name, list(shape), dt))
        tt = T(name, t)
        tt.psum = True
        return tt

    def _deps(self, comp, reads, writes):
        deps = {}

        def add(h, v):
            if deps.get(h, 0) < v:
                deps[h] = v
        for t in reads:
            if t.w is not None:
                add(*t.w)
            if t.psum:
                for h, v in t.r.items():
                    if h is not comp:
                        add(h, v)
        for t in writes:
            if t.w is not None:
                add(*t.w)
            for h, v in t.r.items():
                add(h, v)
        return deps

    def _wait(self, issuer, comp, deps):
        for h, v in deps.items():
            if h is self.pe and comp is self.pe:
                continue
            if issuer.seen.get(h, 0) < v:
                issuer.h.wait_ge(h.sem, v)
                issuer.seen[h] = v

    def _commit(self, comp, reads, writes, inc):
        comp.cnt += inc
        for t in reads:
            t.r[comp] = comp.cnt
        for t in writes:
            t.w = (comp, comp.cnt)
            t.r = {}

    def op(self, eng, fn, reads=(), writes=()):
        if self.muted:
            return None
        deps = self._deps(eng, reads, writes)
        self._wait(eng, eng, deps)
        ins = fn()
        ins.then_inc(eng.sem, 1)
        self._commit(eng, reads, writes, 1)
        return ins

    def group(self, fns, reads=(), writes=()):
        if self.muted:
            return None
        eng = self.pe
        deps = self._deps(eng, reads, writes)
        self._wait(eng, eng, deps)
        ins = None
        for f in fns:
            ins = f()
        ins.then_inc(eng.sem, 1)
        self._commit(eng, reads, writes, 1)

    def dma(self, issuer, out, in_, holder, reads=(), writes=()):
        if self.muted:
            return None
        comp = self.holder(holder, issuer)
        deps = self._deps(comp, reads, writes)
        self._wait(issuer, comp, deps)
        ins = issuer.h.dma_start(out=out, in_=in_)
        ins.then_inc(comp.sem, 16)
        self._commit(comp, reads, writes, 16)
        return ins

    def barrier_all(self):
        if self.muted:
            return None
        holders = self.engs + self.dma_holders
        for e in self.engs:
            for h in holders:
                if h is e:
                    continue
                if h.cnt > 0 and e.seen.get(h, 0) < h.cnt:
                    e.h.wait_ge(h.sem, h.cnt)
                    e.seen[h] = h.cnt


class StopBuild(Exception):
    pass


KSTOP = [None]
KSKIP = set()


FWREF = [None]


def ckpt(name):
    if KSTOP[0] == name:
        FWREF[0].muted = True


def build_program():
    nc = bass.Bass("TRN2", target_bir_lowering=False)

    def din(name, shape):
        return nc.dram_tensor(name, list(shape), F32, kind="ExternalInput").ap()

    def dout(name, shape):
        return nc.dram_tensor(name, list(shape), F32, kind="ExternalOutput").ap()

    xp = din("xp", [SEQ, D])
    xsm = din("xsm", [NS, D])
    ck = din("ck", [NS, 128, 256])
    cv = din("cv", [NS, 128, 256])
    w_in_a = din("w_in_a", [D, 3 * AW])
    w_out_a = din("w_out_a", [AW, D])
    w_kv = din("w_kv", [D, 512])
    w_in_b = din("w_in_b", [D, 2048])
    w_out_b = din("w_out_b", [D, D])
    gaT_d = din("gaT", [128, 8])
    gvT_d = din("gvT", [128, 16])
    gkvT_d = din("gkvT", [128, 8])
    gbT_d = din("gbT", [128, 8])
    gv_row = din("gv_row", [AW])
    gf_row = din("gf_row", [D])
    wsT_d = din("wsT", [128, 8 * 128])
    w00_d = din("w00", [8])
    bs_d = din("bs", [8 * 128])
    bs0_d = din("bs0", [8])
    sinks_d = din("sinks", [16])
    rope_d = din("rope", [SEQ + 128, 16])
    ident_d = din("ident", [128, 128])
    cmask_d = din("cmask", [128, 128])
    mprev_d = din("mprev", [128, 512])
    mcur_d = din("mcur", [128, 512])

    yp = dout("yp", [SEQ, D])
    ysm = dout("ysm", [NS, D])
    nkp = dout("nkp", [128, 256])
    nvp = dout("nvp", [128, 256])
    nks = dout("nks", [NS, 128, 256])
    nvs = dout("nvs", [NS, 128, 256])
    nav = dout("nav", [NS, AW])
    h1s = nc.dram_tensor("h1s", [SEQ + 128, D], F32).ap()

    with contextlib.ExitStack() as es:
        fw = FW(nc, es)
        FWREF[0] = fw
        pe, act, dve, pool, sp = fw.pe, fw.act, fw.dve, fw.pool, fw.sp
        V, S, G, P_ = nc.vector, nc.scalar, nc.gpsimd, nc.tensor

        def rstd_from_ss(ss_ap, tmp_t, out_t, n, width):
            fw.op(dve, lambda: V.tensor_scalar(out=tmp_t[0:n, 0:1], in0=ss_ap, scalar1=1.0 / width, scalar2=EPS,
                                               op0=ALU.mult, op1=ALU.add), reads=[tmp_t.src], writes=[tmp_t])
            fw.op(act, lambda: S.activation(out=tmp_t[0:n, 0:1], in_=tmp_t[0:n, 0:1], func=AF.Ln),
                  reads=[tmp_t], writes=[tmp_t])
            fw.op(act, lambda: S.activation(out=out_t[0:n, 0:1], in_=tmp_t[0:n, 0:1], func=AF.Exp, scale=-0.5),
                  reads=[tmp_t], writes=[out_t])

        def body():
            identb = fw.sb("identb", [128, 128], BF16, dma=True)
            fw.dma(pool, identb[:], ident_d[:, :], identb, writes=[identb])
            gaT = fw.sb("gaT", [128, 8], F32, dma=True)
            fw.dma(sp, gaT[:], gaT_d[:, :], gaT, writes=[gaT])
            gvT = fw.sb("gvT", [128, 16], F32, dma=True)
            fw.dma(sp, gvT[:], gvT_d[:, :], gvT, writes=[gvT])
            gkvT = fw.sb("gkvT", [128, 8], F32, dma=True)
            fw.dma(sp, gkvT[:], gkvT_d[:, :], gkvT, writes=[gkvT])
            gbT = fw.sb("gbT", [128, 8], F32, dma=True)
            fw.dma(sp, gbT[:], gbT_d[:, :], gbT, writes=[gbT])
            onesf = fw.sb("onesf", [128, 8], F32)
            fw.op(pool, lambda: G.memset(onesf[:], 1.0), writes=[onesf])
            ssx = fw.sb("ssx", [128, 1], F32)
            tmx = fw.sb("tmx", [128, 1], F32)
            rsx = fw.sb("rsx", [128, 1], F32)
            tmx.src = ssx
            ssv = fw.sb("ssv", [128, 4], F32)
            ssv1 = fw.sb("ssv1", [128, 1], F32)
            tmv = fw.sb("tmv", [128, 1], F32)
            rsv = fw.sb("rsv", [128, 1], F32)
            tmv.src = ssv1

            WSH = [fw.sb(f"WSH{i}", [128, 8, 512], BF16, dma=True) for i in range(4)]
            with contextlib.ExitStack() as ea:
                WA = [WSH[i - 4] if 4 <= i < 8 else fw.sb(f"WA{i}", [128, 8, 512], BF16, dma=True, es=ea) for i in range(12)]
                WO = [fw.sb(f"WO{i}", [128, 16, 512], BF16, dma=True, es=ea) for i in range(2)]
                wsTm = fw.sb("wsTm", [128, 8, 128], BF16, dma=True, es=ea)
                cmask = fw.sb("cmask", [128, 128], BF16, dma=True, es=ea)
                btile = fw.sb("btile", [128, 8, 128], BF16, es=ea)
                bs0 = fw.sb("bs0", [128, 8], F32, dma=True, es=ea)
                w00 = fw.sb("w00", [128, 8], F32, dma=True, es=ea)
                xs = [fw.sb(f"xs{i}", [128, D], F32, dma=True, es=ea) for i in range(2)]
                hs = [fw.sb(f"hsA{i}", [128, D], F32, dma=True, es=ea) for i in range(2)]
                xsn2 = [fw.sb(f"xsn{i}", [128, D], BF16, es=ea) for i in range(2)]
                xT = fw.sb("xT", [128, 8, 512], BF16, es=ea)
                vraw = [fw.sb("vraw0", [128, AW], BF16, es=ea)]
                wss = [fw.sb("wss0", [128, 8, 128], BF16, es=ea)]
                yT = fw.sb("yT", [128, 16, 512], BF16, es=ea)
                sgb = [fw.sb(f"sgb{i}", [128, 512], BF16, es=ea) for i in range(2)]
                usg = [fw.sb(f"usg{i}", [128, 512], BF16, es=ea) for i in range(2)]
                zzb = [fw.sb(f"zzb{i}", [128, 512], BF16, es=ea) for i in range(2)]
                pa = [fw.ps(f"pa{i}", [128, 512], F32, es=ea) for i in range(8)]
                pav = {id(t_): TV(t_, t_.ap[:].bitcast(BF16).rearrange("p (k t) -> p k t", k=8)) for t_ in pa}
                pac = [0]

                def next_pa():
                    pac[0] += 1
                    return pa[pac[0] % 8]
                eh = ea.enter_context(contextlib.ExitStack())
                vraw += [fw.sb(f"vraw{i}", [128, AW], BF16, es=eh) for i in range(1, 4)]
                wss += [fw.sb(f"wss{i}", [128, 8, 128], BF16, es=eh) for i in range(1, 4)]

                fw.dma(pool, cmask[:], cmask_d[:, :], cmask, writes=[cmask])
                fw.dma(pool, wsTm[:].rearrange("p g i -> p (g i)"), wsT_d[:, :], wsTm, writes=[wsTm])
                fw.dma(sp, xs[0][:], bs_d.partition_broadcast(128), xs[0], writes=[xs[0]])
                fw.op(act, lambda: S.activation(out=btile[:].rearrange("p g i -> p (g i)"), in_=xs[0][:], func=AF.Copy),
                      reads=[xs[0]], writes=[btile])
                fw.dma(sp, bs0[:], bs0_d.partition_broadcast(128), bs0, writes=[bs0])
                fw.dma(sp, w00[:], w00_d.partition_broadcast(128), w00, writes=[w00])
                w_in_v = w_in_a.rearrange("(k p) c -> p k c", p=128)
                for i in [4, 5, 6, 7, 0, 8, 1, 9, 2, 10, 3, 11]:
                    fw.dma(pool, WA[i][:], w_in_v[:, :, i * 512:(i + 1) * 512], WA[i], writes=[WA[i]])
                w_out_v = w_out_a.rearrange("(k p) c -> p k c", p=128)
                for i in range(2):
                    fw.dma(pool, WO[i][:], w_out_v[:, :, i * 512:(i + 1) * 512], WO[i], writes=[WO[i]])
                fw.op(dve, lambda: V.tensor_tensor(out=wsTm[:], in0=wsTm[:],
                                                   in1=cmask[:].unsqueeze(1).broadcast_to([128, 8, 128]), op=ALU.mult),
                      reads=[wsTm, cmask], writes=[wsTm])
                ckpt('consts')
                wd = None

                def front_stats(x_rows, c, nt, sample):
                    xb = xs[c % 2]
                    xsn = xsn2[c % 2]
                    nld = NS if sample else nt
                    fw.dma(sp, xb[0:nld, :], x_rows(c), xb, writes=[xb])
                    fw.op(act, lambda: S.activation(out=xsn[0:nt, :], in_=xb[0:nt, :], func=AF.Square,
                                                    accum_out=ssx[0:nt, :]), reads=[xb], writes=[xsn, ssx])
                    rstd_from_ss(ssx[0:nt, 0:1], tmx, rsx, nt, D)
                    fw.op(act, lambda: S.activation(out=xsn[0:nt, :], in_=xb[0:nt, :], func=AF.Copy,
                                                    scale=rsx[0:nt, 0:1]), reads=[xb, rsx], writes=[xsn])

                xTc = [T(f"xTc{c}", xT.ap[:, :, c * 128:(c + 1) * 128]) for c in range(4)]

                def front_T(c, nt):
                    xsn = xsn2[c % 2]
                    pT = pav[id(next_pa())]
                    fw.group([lambda k=k: P_.transpose(pT[:, k, 0:nt], xsn[0:nt, k * 128:(k + 1) * 128],
                                                       identb[0:nt, 0:nt]) for k in range(8)],
                             reads=[xsn, identb], writes=[pT])
                    fw.op(dve, lambda: V.tensor_tensor(out=xTc[c][:, :, 0:nt], in0=pT[:, :, 0:nt],
                                                       in1=gaT[:].unsqueeze(2).broadcast_to([128, 8, nt]), op=ALU.mult),
                          reads=[pT, gaT], writes=[xTc[c]])

                def block_prologue(x_rows, nch, nt, sample):
                    front_stats(x_rows, 0, nt, sample)
                    if nch > 1:
                        front_stats(x_rows, 1, nt, sample)
                    front_T(0, nt)

                def layer_a_block(x_rows, h_rows, nch, nt, sample, pre_done=False, next_front=None):
                    N = (nch - 1) * 128 + nt
                    if not pre_done:
                        block_prologue(x_rows, nch, nt, sample)
                    for c in range(nch):
                        if c + 2 < nch:
                            front_stats(x_rows, c + 2, nt, sample)
                        if c + 1 < nch:
                            front_T(c + 1, nt)
                        ckpt('s_xT' if sample else 'xT')
                        for cb in range(4):
                            pvb = next_pa()
                            wt = WA[4 + cb]
                            fw.group([lambda k=k: P_.matmul(pvb[0:nt, :], xTc[c][:, k, 0:nt], wt[:, k, :],
                                                            start=(k == 0), stop=(k == 7)) for k in range(8)],
                                     reads=[xTc[c], wt], writes=[pvb])
                            fw.op(act, lambda: S.activation(out=vraw[c][0:nt, cb * 512:(cb + 1) * 512], in_=pvb[0:nt, :],
                                                            func=AF.Square, accum_out=ssv[0:nt, cb:cb + 1]),
                                  reads=[pvb], writes=[vraw[c], ssv])
                            fw.op(act, lambda: S.activation(out=vraw[c][0:nt, cb * 512:(cb + 1) * 512], in_=pvb[0:nt, :],
                                                            func=AF.Copy), reads=[pvb], writes=[vraw[c]])
                            if sample and 'vf32' not in KSKIP:
                                fw.op(act, lambda: S.activation(out=vf32[0:nt, cb * 512:(cb + 1) * 512], in_=pvb[0:nt, :],
                                                                func=AF.Copy), reads=[pvb], writes=[vf32])
                        fw.op(dve, lambda: V.reduce_sum(out=ssv1[0:nt, :], in_=ssv[0:nt, :], axis=AX.X),
                              reads=[ssv], writes=[ssv1])
                        rstd_from_ss(ssv1[0:nt, 0:1], tmv, rsv, nt, AW)
                        wsrc = wd if sample else wsTm
                        fw.op(dve, lambda: V.tensor_scalar(out=wss[c][0:nt, :, 0:nt], in0=wsrc[0:nt, :, 0:nt],
                                                           scalar1=rsv[0:nt, 0:1], scalar2=None, op0=ALU.mult),
                              reads=[wsrc, rsv], writes=[wss[c]])
                        ckpt('s_vproj' if sample else 'vproj')
                        if sample and 'nav' not in KSKIP:
                            fw.op(dve, lambda: V.scalar_tensor_tensor(out=vf32[0:nt, :], in0=vf32[0:nt, :], scalar=rsv[0:nt, 0:1],
                                                                      in1=gvrow[0:nt, :], op0=ALU.mult, op1=ALU.mult),
                                  reads=[vf32, rsv, gvrow], writes=[vf32])
                            fw.dma(sp, nav[:, :], vf32[0:NS, :], vf32, reads=[vf32])
                    if sample:
                        fw.dma(pool, WSH[0][:], w_kv.rearrange("(k p) c -> p k c", p=128), WSH[0], writes=[WSH[0]])
                        w_inb_v0 = w_in_b.rearrange("(k p) c -> p k c", p=128)
                        for i_ in range(3):
                            fw.dma(pool, WSH[1 + i_][:], w_inb_v0[:, :, i_ * 512:(i_ + 1) * 512], WSH[1 + i_],
                                   writes=[WSH[1 + i_]])
                    for cc in range(16):
                        g = cc // 2
                        wu = WA[cc // 4]
                        wg = WA[8 + cc // 4]
                        co = (cc % 4) * 128
                        pg, pu, pz = next_pa(), next_pa(), next_pa()
                        fw.group([lambda k=k: P_.matmul(pg[:, 0:N], wg[:, k, co:co + 128], xT[:, k, 0:N],
                                                        start=(k == 0), stop=(k == 7)) for k in range(8)],
                                 reads=[wg] + xTc[:nch], writes=[pg])
                        fw.group([lambda k=k: P_.matmul(pu[:, 0:N], wu[:, k, co:co + 128], xT[:, k, 0:N],
                                                        start=(k == 0), stop=(k == 7)) for k in range(8)],
                                 reads=[wu] + xTc[:nch], writes=[pu])
                        fw.group([lambda c=c: P_.matmul(pz[:, c * 128:c * 128 + nt], vraw[c][0:nt, cc * 128:(cc + 1) * 128],
                                                        wss[c][0:nt, g, 0:nt], start=True, stop=True) for c in range(nch)],
                                 reads=list(vraw[:nch]) + list(wss[:nch]), writes=[pz])
                        sg_, us_, zz_ = sgb[cc % 2], usg[cc % 2], zzb[cc % 2]
                        fw.op(act, lambda: S.activation(out=sg_[:, 0:N], in_=pg[:, 0:N], func=AF.Silu),
                              reads=[pg], writes=[sg_])
                        fw.op(dve, lambda: V.tensor_tensor(out=us_[:, 0:N], in0=pu[:, 0:N], in1=sg_[:, 0:N], op=ALU.mult),
                              reads=[pu, sg_], writes=[us_])
                        if sample:
                            fw.op(dve, lambda: V.scalar_tensor_tensor(out=zz_[:, 0:N], in0=pz[:, 0:N], scalar=gvT[:, cc:cc + 1],
                                                                      in1=bs0[:, g:g + 1].broadcast_to([128, N]),
                                                                      op0=ALU.mult, op1=ALU.add),
                                  reads=[pz, gvT, bs0], writes=[zz_])
                        else:
                            fw.op(dve, lambda: V.scalar_tensor_tensor(
                                out=zz_[:, 0:N].rearrange("p (c i) -> p c i", c=nch),
                                in0=pz[:, 0:N].rearrange("p (c i) -> p c i", c=nch), scalar=gvT[:, cc:cc + 1],
                                in1=btile[:, g, :].unsqueeze(1).broadcast_to([128, nch, 128]),
                                op0=ALU.mult, op1=ALU.add), reads=[pz, gvT, btile], writes=[zz_])
                        fw.op(pool, lambda: G.tensor_tensor(out=yT[:, cc, 0:N], in0=zz_[:, 0:N], in1=us_[:, 0:N], op=ALU.mult),
                              reads=[zz_, us_], writes=[yT])
                        ckpt('s_cc0' if sample else 'cc0')
                    if next_front is not None:
                        next_front()
                    for c in range(nch):
                        hb = hs[c % 2]
                        fw.dma(sp, hb[0:(NS if sample else nt), :], x_rows(c), hb, writes=[hb])
                        for db in range(2):
                            pob = next_pa()
                            fw.group([lambda cc=cc: P_.matmul(pob[0:nt, :], yT[:, cc, c * 128:c * 128 + nt], WO[db][:, cc, :],
                                                              start=(cc == 0), stop=(cc == 15)) for cc in range(16)],
                                     reads=[yT, WO[db]], writes=[pob])
                            fw.op(dve, lambda: V.tensor_tensor(out=hb[0:nt, db * 512:(db + 1) * 512], in0=pob[0:nt, :],
                                                               in1=hb[0:nt, db * 512:(db + 1) * 512], op=ALU.add),
                                  reads=[pob, hb], writes=[hb])
                        fw.dma(pool, h_rows(c), hb[0:nt, :], hb, reads=[hb])
                        ckpt('s_chunk0' if sample else 'chunk0')

                def xrows(blk):
                    return lambda c: xp[(blk * 4 + c) * 128:(blk * 4 + c + 1) * 128, :]

                for blk in range(4):
                    nf = (lambda blk=blk: block_prologue(xrows(blk + 1), 4, 128, False)) if blk < 3 else None
                    layer_a_block(xrows(blk), lambda c, blk=blk: h1s[(blk * 4 + c) * 128:(blk * 4 + c + 1) * 128, :],
                                  4, 128, False, pre_done=(blk > 0), next_front=nf)
                ckpt('blockA')
                fw.barrier_all()
                eh.close()
                vf32 = fw.sb("vf32", [128, AW], F32, dma=True, es=ea)
                gvrow = fw.sb("gvrow", [128, AW], F32, dma=True, es=ea)
                wd = fw.sb("wd", [128, 8, 128], BF16, es=ea)
                fw.dma(sp, gvrow[:], gv_row.partition_broadcast(128), gvrow, writes=[gvrow])
                fw.op(dve, lambda: V.tensor_tensor(out=wd[:], in0=identb[:].unsqueeze(1).broadcast_to([128, 8, 128]),
                                                   in1=w00[:].unsqueeze(2).broadcast_to([128, 8, 128]), op=ALU.mult),
                      reads=[identb, w00], writes=[wd])
                for tz in (xs[0], hs[0]):
                    fw.op(pool, lambda: G.memset(tz[:], 0.0), writes=[tz])
                layer_a_block(lambda c: xsm[:, :], lambda c: h1s[SEQ:SEQ + 128, :], 1, 128, True)
                ckpt('sampleA')
                fw.barrier_all()

            with contextlib.ExitStack() as eb:
                WKV = WSH[0]
                WQG = [WSH[1], WSH[2], WSH[3], fw.sb("WQG3", [128, 8, 512], BF16, dma=True, es=eb)]
                WOB = [fw.sb(f"WOB{i}", [128, 8, 512], BF16, dma=True, es=eb) for i in range(2)]
                w_inb_v = w_in_b.rearrange("(k p) c -> p k c", p=128)
                fw.dma(pool, WQG[3][:], w_inb_v[:, :, 3 * 512:4 * 512], WQG[3], writes=[WQG[3]])
                w_outb_v = w_out_b.rearrange("(k p) c -> p k c", p=128)
                for i in range(2):
                    fw.dma(pool, WOB[i][:], w_outb_v[:, :, i * 512:(i + 1) * 512], WOB[i], writes=[WOB[i]])
                mprev = fw.sb("mprev", [128, 512], BF16, dma=True, es=eb)
                mcur = fw.sb("mcur", [128, 512], BF16, dma=True, es=eb)
                fw.dma(pool, mprev[:], mprev_d[:, :], mprev, writes=[mprev])
                fw.dma(pool, mcur[:], mcur_d[:, :], mcur, writes=[mcur])
                gfrow = fw.sb("gfrow", [128, D], F32, dma=True, es=eb)
                fw.dma(sp, gfrow[:], gf_row.partition_broadcast(128), gfrow, writes=[gfrow])
                esink = fw.sb("esink", [128, 16], F32, dma=True, es=eb)
                fw.dma(sp, esink[:], sinks_d.partition_broadcast(128), esink, writes=[esink])
                fw.op(act, lambda: S.activation(out=esink[:], in_=esink[:], func=AF.Exp), reads=[esink], writes=[esink])
                onesb = fw.sb("onesb", [128, 128], BF16, es=eb)
                fw.op(pool, lambda: G.memset(onesb[:], 1.0), writes=[onesb])

                hs = [fw.sb(f"hsB{i}", [128, D], F32, dma=True, es=eb) for i in range(4)]
                ys = [fw.sb(f"ysB{i}", [128, D], F32, dma=True, es=eb) for i in range(2)]
                rope = [fw.sb(f"rope{i}", [128, 16], F32, dma=True, es=eb) for i in range(2)]
                hsn = fw.sb("hsn", [128, D], BF16, es=eb)
                hkT = fw.sb("hkT", [128, 8, 128], BF16, es=eb)
                hbT = fw.sb("hbT", [128, 8, 128], BF16, es=eb)
                kr = [fw.sb(f"kr{i}", [128, 4, 64], F32, dma=True, es=eb) for i in range(2)]
                vf = [fw.sb(f"vf{i}", [128, 4, 64], F32, dma=True, es=eb) for i in range(2)]
                ta = fw.sb("ta", [128, 16, 8], F32, es=eb)
                tb = fw.sb("tb", [128, 16, 8], F32, es=eb)
                tc = fw.sb("tc", [128, 16, 8], F32, es=eb)
                td = fw.sb("td", [128, 16, 8], F32, es=eb)
                xf16 = fw.sb("xf16", [128, 8, 16], F32, es=eb)
                egb = [fw.sb(f"egb{i}", [128, 512], F32, es=eb) for i in range(2)]
                xq16 = [fw.sb(f"xq16_{i}", [128, 8, 16], F32, es=eb) for i in range(2)]
                kdup = fw.sb("kdup", [128, 4, 2, 64], BF16, es=eb)
                kTz = [fw.sb(f"kTz{i}", [128, 4, 2, 128], BF16, es=eb) for i in range(3)]
                vaug = [fw.sb(f"vaug{i}", [128, 4, 65], BF16, es=eb) for i in range(3)]
                qr = fw.sb("qr", [128, 16, 64], BF16, es=eb)
                sgt2 = [fw.sb(f"sgt{i}", [128, D], BF16, es=eb) for i in range(2)]
                qT2 = [fw.sb(f"qT{i}", [128, 8, 128], BF16, es=eb) for i in range(2)]
                PT = [fw.sb(f"PT{i}", [128, 512], BF16, es=eb) for i in range(4)]
                den = fw.sb("den", [128, 16], F32, es=eb)
                rden = fw.sb("rden", [128, 16], F32, es=eb)
                on = fw.sb("on", [128, 16, 64], BF16, es=eb)
                og = fw.sb("og", [128, D], BF16, es=eb)
                ogT = fw.sb("ogT", [128, 8, 128], BF16, es=eb)
                pp = [fw.ps(f"pp{i}", [128, 512], F32, es=eb) for i in range(5)]
                ppv = {id(t_): TV(t_, t_.ap[:].bitcast(BF16).rearrange("p (k t) -> p k t", k=8)) for t_ in pp}

                def next_pT():
                    return ppv[id(next_pp())]
                po = [fw.ps(f"poB{i}", [128, 7, 72], F32, es=eb) for i in range(3)]
                for i in range(3):
                    fw.op(pool, lambda: G.memset(kTz[i][:], 0.0), writes=[kTz[i]])
                    fw.op(pool, lambda: G.memset(vaug[i][:], 1.0), writes=[vaug[i]])
                ppc = [0]
                kcA = [fw.sb(f"kcA{i}", [128, 4, 4, 2, 64], BF16, dma=True, es=eb) for i in range(4)]
                vcA = [fw.sb(f"vcA{i}", [128, 4, 4, 2, 64], BF16, dma=True, es=eb) for i in range(4)]
                kTA = [fw.sb(f"kTA{i}", [128, 4, 4, 128], BF16, es=eb) for i in range(4)]
                kct, vct = [], []

                def load_cache_group(i):
                    for (grp, src, lst) in ((kcA[i], ck, kct), (vcA[i], cv, vct)):
                        g4 = []
                        for bb in range(4):
                            tk = T(f"{grp.name}_{bb}", grp.ap[:, bb])
                            tk.dsem_sw = fw.holder(grp, pool)
                            fw.dma(pool, tk[:, :, 0, :], src[4 * i + bb].rearrange("s (h d) -> s h d", h=4), tk, writes=[tk])
                            g4.append(tk)
                        for tk in g4:
                            tk.w = (grp.dsem_sw, grp.dsem_sw.cnt)
                        lst.extend(g4)
                ckpt('b_setup')

                def next_pp():
                    ppc[0] += 1
                    return pp[ppc[0] % 5]

                def rotary(xf, nh, dst, nt, rp, writes_t):
                    x1 = xf[0:nt, 0:nh, 0:8]
                    x2 = xf[0:nt, 0:nh, 8:16]
                    cos = rp[0:nt, 0:8].unsqueeze(1).broadcast_to([nt, nh, 8])
                    sin = rp[0:nt, 8:16].unsqueeze(1).broadcast_to([nt, nh, 8])
                    fw.op(dve, lambda: V.tensor_tensor(out=ta[0:nt, 0:nh, :], in0=x1, in1=cos, op=ALU.mult),
                          reads=[xf, rp], writes=[ta])
                    fw.op(dve, lambda: V.tensor_tensor(out=tb[0:nt, 0:nh, :], in0=x2, in1=sin, op=ALU.mult),
                          reads=[xf, rp], writes=[tb])
                    fw.op(dve, lambda: V.tensor_tensor(out=tc[0:nt, 0:nh, :], in0=x2, in1=cos, op=ALU.mult),
                          reads=[xf, rp], writes=[tc])
                    fw.op(dve, lambda: V.tensor_tensor(out=td[0:nt, 0:nh, :], in0=x1, in1=sin, op=ALU.mult),
                          reads=[xf, rp], writes=[td])
                    fw.op(dve, lambda: V.tensor_tensor(out=dst[:, :, 0:8], in0=ta[0:nt, 0:nh, :], in1=tb[0:nt, 0:nh, :],
                                                       op=ALU.subtract), reads=[ta, tb], writes=[writes_t])
                    fw.op(dve, lambda: V.tensor_tensor(out=dst[:, :, 8:16], in0=tc[0:nt, 0:nh, :], in1=td[0:nt, 0:nh, :],
                                                       op=ALU.add), reads=[tc, td], writes=[writes_t])

                def stage_a_load(ti, nt, rows_in, rope_rows):
                    hb = hs[ti % 4]
                    rp = rope[ti % 2]
                    fw.dma(sp, hb[0:nt, :], rows_in, hb, writes=[hb])
                    fw.dma(sp, rp[0:nt, :], rope_rows, rp, writes=[rp])

                def stage_a(ti, nt):
                    hb = hs[ti % 4]
                    fw.op(act, lambda: S.activation(out=hsn[0:nt, :], in_=hb[0:nt, :], func=AF.Square,
                                                    accum_out=ssx[0:nt, :]), reads=[hb], writes=[hsn, ssx])
                    rstd_from_ss(ssx[0:nt, 0:1], tmx, rsx, nt, D)
                    fw.op(act, lambda: S.activation(out=hsn[0:nt, :], in_=hb[0:nt, :], func=AF.Copy,
                                                    scale=rsx[0:nt, 0:1]), reads=[hb, rsx], writes=[hsn])

                def stage_b1(ti, nt):
                    pT = next_pT()
                    fw.group([lambda k=k: P_.transpose(pT[:, k, 0:nt], hsn[0:nt, k * 128:(k + 1) * 128],
                                                       identb[0:nt, 0:nt]) for k in range(8)],
                             reads=[hsn, identb], writes=[pT])
                    fw.op(dve, lambda: V.tensor_tensor(out=hkT[:, :, 0:nt], in0=pT[:, :, 0:nt],
                                                       in1=gkvT[:].unsqueeze(2).broadcast_to([128, 8, nt]), op=ALU.mult),
                          reads=[pT, gkvT], writes=[hkT])
                    fw.op(dve, lambda: V.tensor_tensor(out=hbT[:, :, 0:nt], in0=pT[:, :, 0:nt],
                                                       in1=gbT[:].unsqueeze(2).broadcast_to([128, 8, nt]), op=ALU.mult),
                          reads=[pT, gbT], writes=[hbT])

                def stage_b2(ti, nt, par, sample):
                    sgt = sgt2[ti % 2]
                    par3 = ti % 3
                    rp = rope[ti % 2]
                    pk = next_pp()
                    fw.group([lambda k=k: P_.matmul(pk[0:nt, :], hkT[:, k, 0:nt], WKV[:, k, :], start=(k == 0), stop=(k == 7))
                              for k in range(8)], reads=[hkT, WKV], writes=[pk])
                    pqs = []
                    for cb in range(2 if sample else 4):
                        pq = next_pp() if cb < 2 else None
                        pqs.append(pq)
                    krb, vfb = kr[par], vf[par]
                    fw.op(act, lambda: S.activation(out=krb[0:nt].rearrange("p h d -> p (h d)"), in_=pk[0:nt, 0:256],
                                                    func=AF.Copy), reads=[pk], writes=[krb])
                    fw.op(act, lambda: S.activation(out=xf16[0:nt, 0:4, :],
                                                    in_=pk[0:nt, 0:256].rearrange("p (h d) -> p h d", h=4)[:, :, 0:16],
                                                    func=AF.Copy), reads=[pk], writes=[xf16])
                    fw.op(act, lambda: S.activation(out=vfb[0:nt].rearrange("p h d -> p (h d)"), in_=pk[0:nt, 256:512],
                                                    func=AF.Copy), reads=[pk], writes=[vfb])
                    fw.op(act, lambda: S.activation(out=vaug[par3][0:nt, :, 0:64],
                                                    in_=pk[0:nt, 256:512].rearrange("p (h d) -> p h d", h=4), func=AF.Copy),
                          reads=[pk], writes=[vaug[par3]])
                    rotary(xf16, 4, krb[0:nt], nt, rp, krb)
                    fw.op(pool, lambda: G.tensor_copy(out=kdup[0:nt], in_=krb[0:nt].unsqueeze(2).broadcast_to([nt, 4, 2, 64])),
                          reads=[krb], writes=[kdup])
                    for cb in range(2):
                        pq = pqs[cb]
                        fw.group([lambda k=k: P_.matmul(pq[0:nt, :], hbT[:, k, 0:nt], WQG[cb][:, k, :],
                                                        start=(k == 0), stop=(k == 7)) for k in range(8)],
                                 reads=[hbT, WQG[cb]], writes=[pq])
                        fw.op(act, lambda: S.activation(out=qr[0:nt, cb * 8:(cb + 1) * 8].rearrange("p h d -> p (h d)"),
                                                        in_=pq[0:nt, :], func=AF.Copy), reads=[pq], writes=[qr])
                        xq = xq16[cb]
                        fw.op(act, lambda: S.activation(out=xq[0:nt, :, :],
                                                        in_=pq[0:nt, :].rearrange("p (h d) -> p h d", h=8)[:, :, 0:16],
                                                        func=AF.Copy), reads=[pq], writes=[xq])
                        rotary(xq, 8, qr[0:nt, cb * 8:(cb + 1) * 8], nt, rp, qr)
                    if not sample:
                        for cb in range(2):
                            pq = next_pp()
                            fw.group([lambda k=k: P_.matmul(pq[0:nt, :], hbT[:, k, 0:nt], WQG[2 + cb][:, k, :],
                                                            start=(k == 0), stop=(k == 7)) for k in range(8)],
                                     reads=[hbT, WQG[2 + cb]], writes=[pq])
                            fw.op(act, lambda: S.activation(out=sgt[0:nt, cb * 512:(cb + 1) * 512], in_=pq[0:nt, :],
                                                            func=AF.Silu), reads=[pq], writes=[sgt])

                def stage_b3(ti, nt):
                    qT = qT2[ti % 2]
                    par3 = ti % 3
                    pT = next_pT()
                    fw.group([lambda kh=kh: P_.transpose(pT[:, kh, 0:nt], kdup[0:nt, kh].rearrange("p r d -> p (r d)"),
                                                         identb[0:nt, 0:nt]) for kh in range(4)],
                             reads=[kdup, identb], writes=[pT])
                    fw.op(act, lambda: S.activation(out=kTz[par3][0:64, :, 0, 0:nt], in_=pT[0:64, 0:4, 0:nt], func=AF.Copy),
                          reads=[pT], writes=[kTz[par3]])
                    fw.op(act, lambda: S.activation(out=kTz[par3][64:128, :, 1, 0:nt], in_=pT[64:128, 0:4, 0:nt], func=AF.Copy),
                          reads=[pT], writes=[kTz[par3]])
                    pT = next_pT()
                    fw.group([lambda m=m: P_.transpose(pT[:, m, 0:nt], qr[0:nt, 2 * m:2 * m + 2].rearrange("p h d -> p (h d)"),
                                                       identb[0:nt, 0:nt]) for m in range(8)],
                             reads=[qr, identb], writes=[pT])
                    fw.op(dve, lambda: V.tensor_tensor(out=qT[:, :, 0:nt], in0=pT[:, :, 0:nt],
                                                       in1=onesf[:].unsqueeze(2).broadcast_to([128, 8, nt]), op=ALU.mult),
                          reads=[pT, onesf], writes=[qT])

                def layer_b_tail(ti, nt, hb, rows_out, n_out=128):
                    for db in range(2):
                        pob = next_pp()
                        fw.group([lambda m=m: P_.matmul(pob[0:nt, :], ogT[:, m, 0:nt], WOB[db][:, m, :],
                                                        start=(m == 0), stop=(m == 7)) for m in range(8)],
                                 reads=[ogT, WOB[db]], writes=[pob])
                        fw.op(dve, lambda: V.tensor_tensor(out=hb[0:nt, db * 512:(db + 1) * 512], in0=pob[0:nt, :],
                                                           in1=hb[0:nt, db * 512:(db + 1) * 512], op=ALU.add),
                              reads=[pob, hb], writes=[hb])
                    yb = ys[ti % 2]
                    fw.op(act, lambda: S.activation(out=yb[0:nt, :], in_=hb[0:nt, :], func=AF.Square,
                                                    accum_out=ssv1[0:nt, :]), reads=[hb], writes=[yb, ssv1])
                    rstd_from_ss(ssv1[0:nt, 0:1], tmv, rsv, nt, D)
                    fw.op(dve, lambda: V.scalar_tensor_tensor(out=yb[0:nt, :], in0=hb[0:nt, :], scalar=rsv[0:nt, 0:1],
                                                              in1=gfrow[0:nt, :], op0=ALU.mult, op1=ALU.mult),
                          reads=[hb, rsv, gfrow], writes=[yb])
                    fw.dma(pool, rows_out, yb[0:n_out, :], yb, reads=[yb])

                def attn_scores(ti, kh):
                    qT = qT2[ti % 2]
                    kbs = ([((ti - 1) % 3, mprev)] if ti > 0 else []) + [(ti % 3, mcur)]
                    pts = []
                    for kbi, (kpar, msk) in enumerate(kbs):
                        psb = next_pp()
                        fw.group([
                            lambda: P_.matmul(psb[:, :], identb[:], msk[:], start=True, stop=False),
                            lambda: P_.matmul(psb[:, 0:256], kTz[kpar][:, kh, 0, :],
                                              qT[:, 2 * kh:2 * kh + 2, :].rearrange("p m q -> p (m q)"),
                                              start=False, stop=False),
                            lambda: P_.matmul(psb[:, 256:512], kTz[kpar][:, kh, 1, :],
                                              qT[:, 2 * kh:2 * kh + 2, :].rearrange("p m q -> p (m q)"),
                                              start=False, stop=True),
                        ], reads=[identb, msk, kTz[kpar], qT], writes=[psb])
                        ptb = PT[(kh % 2) * 2 + kbi]
                        fw.op(act, lambda: S.activation(out=ptb[:], in_=psb[:], func=AF.Exp, scale=0.125),
                              reads=[psb], writes=[ptb])
                        pts.append((ptb, kpar))
                    return pts

                def attn_pv(ti, kh, pts):
                    for r in range(2):
                        for mm in range(2):
                            h = 4 * kh + 2 * mm + r
                            bank, slot = po[h // 7], h % 7
                            c0 = r * 256 + mm * 128
                            fw.group([lambda i=i: P_.matmul(bank[:, slot, 0:65], pts[i][0][:, c0:c0 + 128],
                                                            vaug[pts[i][1]][:, kh, :], start=(i == 0),
                                                            stop=(i == len(pts) - 1)) for i in range(len(pts))],
                                     reads=[p[0] for p in pts] + [vaug[p[1]] for p in pts], writes=[bank])

                def attn_finish(ti):
                    sgt = sgt2[ti % 2]
                    for b in range(3):
                        h0 = 7 * b
                        nh = min(7, 16 - h0)
                        fw.op(dve, lambda: V.tensor_tensor(out=den[:, h0:h0 + nh].unsqueeze(2), in0=po[b][:, 0:nh, 64:65],
                                                           in1=esink[:, h0:h0 + nh].unsqueeze(2), op=ALU.add),
                              reads=[po[b], esink], writes=[den])
                    fw.op(dve, lambda: V.reciprocal(out=rden[:], in_=den[:]), reads=[den], writes=[rden])
                    for b in range(3):
                        h0 = 7 * b
                        nh = min(7, 16 - h0)
                        fw.op(dve, lambda: V.tensor_tensor(out=on[:, h0:h0 + nh, :], in0=po[b][:, 0:nh, 0:64],
                                                           in1=rden[:, h0:h0 + nh].unsqueeze(2).broadcast_to([128, nh, 64]),
                                                           op=ALU.mult), reads=[po[b], rden], writes=[on])
                    fw.op(pool, lambda: G.tensor_tensor(out=og[:], in0=on[:].rearrange("p h d -> p (h d)"), in1=sgt[:],
                                                        op=ALU.mult), reads=[on, sgt], writes=[og])
                    pT = next_pT()
                    fw.group([lambda m=m: P_.transpose(pT[:, m, :], og[:, m * 128:(m + 1) * 128], identb[:]) for m in range(8)],
                             reads=[og, identb], writes=[pT])
                    fw.op(act, lambda: S.activation(out=ogT[:], in_=pT[:], func=AF.Copy), reads=[pT], writes=[ogT])

                def rows(ti):
                    if ti == NT:
                        return h1s[SEQ:SEQ + 128, :], rope_d[SEQ:SEQ + 128, :]
                    return h1s[ti * 128:(ti + 1) * 128, :], rope_d[ti * 128:(ti + 1) * 128, :]

                def attn_og(ti):
                    sgt = sgt2[ti % 2]
                    for b in range(3):
                        h0 = 7 * b
                        nh = min(7, 16 - h0)
                        fw.op(dve, lambda: V.tensor_tensor(out=den[:, h0:h0 + nh].unsqueeze(2), in0=po[b][:, 0:nh, 64:65],
                                                           in1=esink[:, h0:h0 + nh].unsqueeze(2), op=ALU.add),
                              reads=[po[b], esink], writes=[den])
                    fw.op(dve, lambda: V.reciprocal(out=rden[:], in_=den[:]), reads=[den], writes=[rden])
                    for b in range(3):
                        h0 = 7 * b
                        nh = min(7, 16 - h0)
                        fw.op(dve, lambda: V.tensor_tensor(out=on[:, h0:h0 + nh, :], in0=po[b][:, 0:nh, 0:64],
                                                           in1=rden[:, h0:h0 + nh].unsqueeze(2).broadcast_to([128, nh, 64]),
                                                           op=ALU.mult), reads=[po[b], rden], writes=[on])
                    fw.op(pool, lambda: G.tensor_tensor(out=og[:], in0=on[:].rearrange("p h d -> p (h d)"), in1=sgt[:],
                                                        op=ALU.mult), reads=[on, sgt], writes=[og])

                def og_transpose():
                    pT = next_pT()
                    fw.group([lambda m=m: P_.transpose(pT[:, m, :], og[:, m * 128:(m + 1) * 128], identb[:]) for m in range(8)],
                             reads=[og, identb], writes=[pT])
                    fw.op(act, lambda: S.activation(out=ogT[:], in_=pT[:], func=AF.Copy), reads=[pT], writes=[ogT])

                stage_a_load(0, 128, *rows(0))
                stage_a(0, 128)
                for i in range(-1, NT + 1):
                    nx = i + 1
                    have_nx = nx <= NT
                    cur = 0 <= i < NT
                    if i + 2 <= NT:
                        stage_a_load(i + 2, 128, *rows(i + 2))
                    if 0 <= i < 4:
                        load_cache_group(i)
                    if 6 <= i < 14:
                        gi = (i - 6) // 2
                        ca = (kcA if i % 2 == 0 else vcA)[gi]
                        cts = (kct if i % 2 == 0 else vct)[4 * gi:4 * gi + 4]
                        fw.op(pool, lambda: G.tensor_copy(out=ca[:, :, :, 1, :], in_=ca[:, :, :, 0, :]), reads=cts, writes=cts)
                    if i in (7, 9, 11, 13):
                        gi = (i - 7) // 2
                        for bb in range(4):
                            kcb = kct[4 * gi + bb]
                            pT = next_pT()
                            fw.group([lambda kh=kh: P_.transpose(pT[:, kh, :], kcb[:, kh].rearrange("p r d -> p (r d)"),
                                                                 identb[:]) for kh in range(4)], reads=[kcb, identb], writes=[pT])
                            if bb % 2 == 0:
                                fw.op(act, lambda: S.activation(out=kTA[gi][:, bb], in_=pT[:, 0:4, :], func=AF.Copy),
                                      reads=[pT], writes=[kTA[gi]])
                            else:
                                fw.op(dve, lambda: V.tensor_tensor(out=kTA[gi][:, bb], in0=pT[:, 0:4, :],
                                                                   in1=onesf[:, 0:4].unsqueeze(2).broadcast_to([128, 4, 128]),
                                                                   op=ALU.mult), reads=[pT, onesf], writes=[kTA[gi]])
                    if have_nx:
                        stage_b1(nx, 128)
                    if cur:
                        p0 = attn_scores(i, 0)
                        p1 = attn_scores(i, 1)
                    if have_nx:
                        stage_b2(nx, 128, nx % 2, nx == NT)
                        if nx == NT - 1:
                            kp = nx % 2
                            fw.dma(sp, nkp[:, :], kr[kp][:].rearrange("p h d -> p (h d)"), kr[kp], reads=[kr[kp]])
                            fw.dma(sp, nvp[:, :], vf[kp][:].rearrange("p h d -> p (h d)"), vf[kp], reads=[vf[kp]])
                    if i >= 1:
                        og_transpose()
                    if cur:
                        attn_pv(i, 0, p0)
                        p2 = attn_scores(i, 2)
                        attn_pv(i, 1, p1)
                        p3 = attn_scores(i, 3)
                    if i >= 1:
                        layer_b_tail(i - 1, 128, hs[(i - 1) % 4], yp[(i - 1) * 128:i * 128, :])
                    if nx + 1 <= NT:
                        stage_a(nx + 1, 128)
                    if cur:
                        attn_pv(i, 2, p2)
                        attn_pv(i, 3, p3)
                    if have_nx:
                        stage_b3(nx, 128)
                    if cur:
                        attn_og(i)

                ckpt('b_prompt')
                with contextlib.ExitStack() as e2:
                    PTs = fw.sb("PTs", [128, NS, 16], BF16, es=e2)
                    sgT = fw.sb("sgT", [128, 8, NS], BF16, es=e2)
                    prod = fw.sb("prod", [128, 16, 64], F32, es=e2)
                    snew = fw.sb("snew", [128, 16], F32, es=e2)
                    pm = fw.sb("pm", [128, NS, 16], BF16, es=e2)
                    vnd = fw.sb("vnd", [128, 4, 2, 64], BF16, es=e2)
                    onew = fw.sb("onew", [128, NS, 16], F32, es=e2)
                    dens = fw.sb("dens", [128, NS, 16], F32, es=e2)
                    osb = fw.sb("osb", [128, NS, 16], F32, es=e2)
                    cpy = T("cpy")
                    fw.add_dsem(cpy)
                    par = 0
                    hb = hs[NT % 4]
                    qT = qT2[NT % 2]
                    krb, vfb = kr[par], vf[par]
                    fw.dma(sp, nks[:, 0:127, :], ck[:, 1:128, :], cpy)
                    fw.dma(sp, nvs[:, 0:127, :], cv[:, 1:128, :], cpy)
                    fw.dma(sp, nks[:, 127, :], krb[0:NS].rearrange("p h d -> p (h d)"), kr[par], reads=[krb])
                    fw.dma(sp, nvs[:, 127, :], vfb[0:NS].rearrange("p h d -> p (h d)"), vf[par], reads=[vfb])
                    pgs = next_pp()
                    for m in range(8):
                        wq = WQG[2 + m // 4]
                        co = (m % 4) * 128
                        fw.group([lambda k=k: P_.matmul(pgs[:, m * NS:(m + 1) * NS], wq[:, k, co:co + 128], hbT[:, k, 0:NS],
                                                        start=(k == 0), stop=(k == 7)) for k in range(8)],
                                 reads=[wq, hbT], writes=[pgs])
                    fw.op(act, lambda: S.activation(out=sgT[:].rearrange("p m b -> p (m b)"), in_=pgs[:, 0:8 * NS],
                                                    func=AF.Silu), reads=[pgs], writes=[sgT])
                    ckpt('s_front')
                    qz = fw.sb("qz", [128, 2, 8, NS], BF16, es=e2)
                    fw.op(pool, lambda: G.memset(qz[:], 0.0), writes=[qz])
                    for r_ in range(2):
                        rw = slice(64 * r_, 64 * r_ + 64)
                        fw.op(dve, lambda: V.tensor_copy(out=qz[rw, r_, :, :], in_=qT[rw, :, 0:NS]), reads=[qT], writes=[qz])
                    pSs = next_pp()
                    for b in range(NS):
                        kta = kTA[b // 4]
                        fns = []
                        for kh in range(4):
                            for r in range(2):
                                fns.append(lambda kh=kh, r=r: P_.matmul(
                                    pSs[:, b * 16 + 4 * kh + r:b * 16 + 4 * kh + r + 3:2], kta[:, b % 4, kh, :],
                                    qz[:, r, 2 * kh:2 * kh + 2, b], start=True, stop=True))
                        fw.group(fns, reads=[kta, qz], writes=[pSs])
                    fw.op(act, lambda: S.activation(out=PTs[:].rearrange("p b h -> p (b h)"), in_=pSs[:, 0:NS * 16],
                                                    func=AF.Exp, scale=0.125), reads=[pSs], writes=[PTs])
                    ckpt('s_scores')
                    pO = next_pp()
                    for b in range(NS):
                        vcb = vct[b]
                        fw.group([lambda kh=kh: P_.matmul(pO[:, b * 16 + 4 * kh:b * 16 + 4 * kh + 4],
                                                          vcb[:, kh].rearrange("p r d -> p (r d)"),
                                                          PTs[:, b, 4 * kh:4 * kh + 4], start=True, stop=True)
                                  for kh in range(4)], reads=[vcb, PTs], writes=[pO])
                    pD = next_pp()
                    fw.group([lambda: P_.matmul(pD[:, 0:NS * 16], onesb[:], PTs[:].rearrange("p b h -> p (b h)"),
                                                start=True, stop=True)], reads=[onesb, PTs], writes=[pD])
                    fw.op(act, lambda: S.activation(out=osb[:].rearrange("p b h -> p (b h)"), in_=pO[:, 0:NS * 16],
                                                    func=AF.Copy), reads=[pO], writes=[osb])
                    fw.op(act, lambda: S.activation(out=dens[:].rearrange("p b h -> p (b h)"), in_=pD[:, 0:NS * 16],
                                                    func=AF.Copy), reads=[pD], writes=[dens])
                    ckpt('s_pv')
                    fw.op(dve, lambda: V.tensor_tensor(out=prod[:].rearrange("p (k g) d -> p k g d", k=4),
                                                       in0=qr[:].rearrange("p (k g) d -> p k g d", k=4),
                                                       in1=krb[:].unsqueeze(2).broadcast_to([128, 4, 4, 64]), op=ALU.mult),
                          reads=[qr, krb], writes=[prod])
                    fw.op(dve, lambda: V.reduce_sum(out=snew[:], in_=prod[:], axis=AX.X), reads=[prod], writes=[snew])
                    fw.op(act, lambda: S.activation(out=snew[:], in_=snew[:], func=AF.Exp, scale=0.125),
                          reads=[snew], writes=[snew])
                    pm4 = pm[:].rearrange("p b h -> p (b h)").rearrange("p (k b g) -> p k b g", k=4, b=NS)
                    fw.op(dve, lambda: V.tensor_tensor(
                        out=pm4, in0=identb[:, 0:NS].unsqueeze(1).unsqueeze(3).broadcast_to([128, 4, NS, 4]),
                        in1=snew[:].rearrange("p (k g) -> p k g", k=4).unsqueeze(2).broadcast_to([128, 4, NS, 4]),
                        op=ALU.mult), reads=[identb, snew], writes=[pm])
                    fw.op(dve, lambda: V.tensor_copy(out=vnd[:], in_=vfb[:].unsqueeze(2).broadcast_to([128, 4, 2, 64])),
                          reads=[vfb], writes=[vnd])
                    pN = next_pp()
                    pmf = pm[:].rearrange("p b h -> p (b h)")
                    fw.group([lambda kh=kh: P_.matmul(
                        pN[:, kh * 64:(kh + 1) * 64], vnd[:, kh].rearrange("p r d -> p (r d)"),
                        pmf[:, kh * 64:(kh + 1) * 64], start=True, stop=True)
                        for kh in range(4)], reads=[vnd, pm], writes=[pN])
                    osb4 = osb[:].rearrange("p b (k g) -> p b k g", k=4)
                    fw.op(dve, lambda: V.tensor_tensor(
                        out=osb4, in0=pN[:, 0:NS * 16].rearrange("p (k b g) -> p b k g", k=4, b=NS), in1=osb4,
                        op=ALU.add), reads=[pN, osb], writes=[osb])
                    pN2 = next_pp()
                    fw.group([lambda: P_.matmul(pN2[:, 0:NS * 16], onesb[:], pm[:].rearrange("p b h -> p (b h)"),
                                                start=True, stop=True)], reads=[onesb, pm], writes=[pN2])
                    dens4 = dens[:].rearrange("p b (k g) -> p b k g", k=4)
                    fw.op(dve, lambda: V.tensor_tensor(
                        out=dens4, in0=pN2[:, 0:NS * 16].rearrange("p (k b g) -> p b k g", k=4, b=NS), in1=dens4,
                        op=ALU.add), reads=[pN2, dens], writes=[dens])
                    fw.op(dve, lambda: V.tensor_tensor(out=dens[:], in0=dens[:],
                                                       in1=esink[:].unsqueeze(1).broadcast_to([128, NS, 16]), op=ALU.add),
                          reads=[dens, esink], writes=[dens])
                    fw.op(dve, lambda: V.reciprocal(out=dens[:], in_=dens[:]), reads=[dens], writes=[dens])
                    fw.op(dve, lambda: V.tensor_tensor(out=osb[:], in0=osb[:], in1=dens[:], op=ALU.mult),
                          reads=[osb, dens], writes=[osb])
                    ckpt('s_new')
                    for r in range(2):
                        rows = slice(64 * r, 64 * r + 64)
                        fw.op(dve, lambda: V.tensor_tensor(
                            out=ogT[rows, :, 0:NS],
                            in0=osb[rows].rearrange("p b (m r) -> p r m b", r=2)[:, r],
                            in1=sgT[rows], op=ALU.mult), reads=[osb, sgT], writes=[ogT])
                    layer_b_tail(NT, 128, hb, ysm[:, :], NS)

        try:
            body()
        except StopBuild:
            pass
        for h in fw.engs + fw.dma_holders:
            if h is not sp and h.cnt > 0 and sp.seen.get(h, 0) < h.cnt:
                sp.h.wait_ge(h.sem, h.cnt)
                sp.seen[h] = h.cnt
    return nc


def _consts():
    pos = np.concatenate([np.arange(SEQ), np.full(128, 8192)]).astype(np.float32)
    inv = (np.float32(500000.0) ** (-np.arange(0, 16, 2, dtype=np.float32) / np.float32(16))).astype(np.float32)
    ang = (pos[:, None] * inv[None, :]).astype(np.float32)
    rope = np.concatenate([np.cos(ang), np.sin(ang)], axis=1).astype(np.float32)
    ident = np.eye(128, dtype=np.float32)
    j = np.arange(128)[:, None]
    i = np.arange(128)[None, :]
    cmask = (j <= i).astype(np.float32)
    mprev = np.where(j >= i, 0.0, NEG).astype(np.float32)
    mcur = np.where(j <= i, 0.0, NEG).astype(np.float32)
    return dict(rope=rope, ident=ident, cmask=cmask, mprev=np.tile(mprev, (1, 4)), mcur=np.tile(mcur, (1, 4)))


_NC_CACHE = {}


def kernel(x_prompt, x_sample, cache_k, cache_v, norm_a, w_in_a, v_norm_a, w_s_a, b_s_a, w_out_a,
           kv_norm, w_kv, norm_b, w_in_b, sinks_b, w_out_b, final_norm):
    f = lambda a: np.ascontiguousarray(np.asarray(a, dtype=np.float32))
    colT = lambda v, k: f(np.asarray(v).reshape(k, 128).T)
    shared = dict(
        w_in_a=f(np.asarray(w_in_a)[0]), w_out_a=f(np.asarray(w_out_a)[0]), w_kv=f(w_kv),
        w_in_b=f(np.asarray(w_in_b)[0]), w_out_b=f(np.asarray(w_out_b)[0]),
        gaT=colT(norm_a, 8), gvT=colT(v_norm_a, 16), gkvT=colT(kv_norm, 8), gbT=colT(norm_b, 8),
        gv_row=f(np.asarray(v_norm_a).reshape(-1)), gf_row=f(np.asarray(final_norm).reshape(-1)),
        wsT=f(np.transpose(np.asarray(w_s_a)[0], (2, 0, 1)).reshape(128, 1024)),
        w00=f(np.asarray(w_s_a)[0, :, 0, 0]), bs=f(np.asarray(b_s_a)[0].reshape(-1)),
        bs0=f(np.asarray(b_s_a)[0, :, 0]), sinks=f(np.asarray(sinks_b).reshape(-1)),
    )
    shared.update(_consts())
    xp = np.asarray(x_prompt, dtype=np.float32)
    xs = np.asarray(x_sample, dtype=np.float32).reshape(128, D)
    ck = np.asarray(cache_k, dtype=np.float32).reshape(128, 128, 256)
    cv = np.asarray(cache_v, dtype=np.float32).reshape(128, 128, 256)
    in_maps = []
    for c in range(NCORES):
        m = dict(shared)
        m["xp"] = f(xp[c])
        m["xsm"] = f(xs[c * NS:(c + 1) * NS])
        m["ck"] = f(ck[c * NS:(c + 1) * NS])
        m["cv"] = f(cv[c * NS:(c + 1) * NS])
        in_maps.append(m)
    if "nc" not in _NC_CACHE:
        _NC_CACHE["nc"] = build_program()
    res = run_bass_kernel_spmd(_NC_CACHE["nc"], in_maps, core_ids=list(range(NCORES)))
    R = res.results
    y_prompt = np.stack([R[c]["yp"] for c in range(NCORES)]).astype(np.float32)
    y_sample = np.concatenate([R[c]["ysm"] for c in range(NCORES)]).reshape(128, 1, D).astype(np.float32)
    nk_p = np.stack([R[c]["nkp"] for c in range(NCORES)]).reshape(8, 128, 4, 64).astype(np.float32)
    nv_p = np.stack([R[c]["nvp"] for c in range(NCORES)]).reshape(8, 128, 4, 64).astype(np.float32)
    nk_s = np.concatenate([R[c]["nks"] for c in range(NCORES)]).reshape(128, 128, 4, 64).astype(np.float32)
    nv_s = np.concatenate([R[c]["nvs"] for c in range(NCORES)]).reshape(128, 128, 4, 64).astype(np.float32)
    nav = np.concatenate([R[c]["nav"] for c in range(NCORES)]).reshape(1, 128, 1, AW).astype(np.float32)
    return (y_prompt, y_sample, nk_p, nv_p, nk_s, nv_s, nav)
```

```python
import contextlib
import numpy as np
import concourse.bass as bass
import concourse.mybir as mybir
from concourse.bass_utils import run_bass_kernel_spmd

F32 = mybir.dt.float32
BF16 = mybir.dt.bfloat16
AF = mybir.ActivationFunctionType
ALU = mybir.AluOpType
AX = mybir.AxisListType

NCORES = 8
D = 1024
SEQ = 2048
NT = SEQ // 128
NS = 16
AW = 2048
EPS = 1e-5
NEG = -30000.0


class Eng:
    def __init__(self, name, h, sem):
        self.name = name
        self.h = h
        self.sem = sem
        self.cnt = 0
        self.seen = {}


class T:
    def __init__(self, name, ap=None):
        self.name = name
        self.ap = ap
        self.w = None
        self.r = {}
        self.dsem = None
        self.dsem_sw = None
        self.psum = False

    def __getitem__(self, k):
        return self.ap[k]


class TV:
    def __init__(self, base, ap):
        object.__setattr__(self, "base", base)
        object.__setattr__(self, "ap", ap)

    def __getattr__(self, k):
        return getattr(object.__getattribute__(self, "base"), k)

    def __setattr__(self, k, v):
        setattr(object.__getattribute__(self, "base"), k, v)

    def __getitem__(self, k):
        return object.__getattribute__(self, "ap")[k]


class FW:
    def __init__(self, nc, es):
        self.nc = nc
        self.es = es
        self.nsem = 0
        mk = lambda n, h: Eng(n, h, self.new_sem(n))
        self.pe = mk("pe", nc.tensor)
        self.act = mk("act", nc.scalar)
        self.dve = mk("dve", nc.vector)
        self.pool = mk("pool", nc.gpsimd)
        self.sp = mk("sp", nc.sync)
        self.engs = [self.pe, self.act, self.dve, self.pool, self.sp]
        self.dma_holders = []
        self.muted = False

    def new_sem(self, name):
        self.nsem += 1
        return self.es.enter_context(self.nc.semaphore(f"s{self.nsem}_{name}"))

    def sb(self, name, shape, dt, dma=False, es=None):
        t = (es or self.es).enter_context(self.nc.sbuf_tensor("sb_" + name, list(shape), dt))
        tt = T(name, t)
        if dma:
            self.add_dsem(tt)
        return tt

    def add_dsem(self, tt):
        pass

    def holder(self, tt, issuer):
        attr = "dsem_sw" if issuer is self.pool else "dsem"
        h = getattr(tt, attr, None)
        if h is None:
            h = Eng(attr + "_" + tt.name, None, self.new_sem("d"))
            setattr(tt, attr, h)
            self.dma_holders.append(h)
        return h

    def ps(self, name, shape, dt, es=None):
        t = (es or self.es).enter_context(self.nc.psum_tensor("ps_" + name, list(shape), dt))
        tt = T(name, t)
        tt.psum = True
        return tt

    def _deps(self, comp, reads, writes):
        deps = {}

        def add(h, v):
            if deps.get(h, 0) < v:
                deps[h] = v
        for t in reads:
            if t.w is not None:
                add(*t.w)
            if t.psum:
                for h, v in t.r.items():
                    if h is not comp:
                        add(h, v)
        for t in writes:
            if t.w is not None:
                add(*t.w)
            for h, v in t.r.items():
                add(h, v)
        return deps

    def _wait(self, issuer, comp, deps):
        for h, v in deps.items():
            if h is self.pe and comp is self.pe:
                continue
            if issuer.seen.get(h, 0) < v:
                issuer.h.wait_ge(h.sem, v)
                issuer.seen[h] = v

    def _commit(self, comp, reads, writes, inc):
        comp.cnt += inc
        for t in reads:
            t.r[comp] = comp.cnt
        for t in writes:
            t.w = (comp, comp.cnt)
            t.r = {}

    def op(self, eng, fn, reads=(), writes=()):
        if self.muted:
            return None
        deps = self._deps(eng, reads, writes)
        self._wait(eng, eng, deps)
        ins = fn()
        ins.then_inc(eng.sem, 1)
        self._commit(eng, reads, writes, 1)
        return ins

    def group(self, fns, reads=(), writes=()):
        if self.muted:
            return None
        eng = self.pe
        deps = self._deps(eng, reads, writes)
        self._wait(eng, eng, deps)
        ins = None
        for f in fns:
            ins = f()
        ins.then_inc(eng.sem, 1)
        self._commit(eng, reads, writes, 1)

    def dma(self, issuer, out, in_, holder, reads=(), writes=()):
        if self.muted:
            return None
        comp = self.holder(holder, issuer)
        deps = self._deps(comp, reads, writes)
        self._wait(issuer, comp, deps)
        ins = issuer.h.dma_start(out=out, in_=in_)
        ins.then_inc(comp.sem, 16)
        self._commit(comp, reads, writes, 16)
        return ins

    def barrier_all(self):
        if self.muted:
            return None
        holders = self.engs + self.dma_holders
        for e in self.engs:
            for h in holders:
                if h is e:
                    continue
                if h.cnt > 0 and e.seen.get(h, 0) < h.cnt:
                    e.h.wait_ge(h.sem, h.cnt)
                    e.seen[h] = h.cnt


class StopBuild(Exception):
    pass


KSTOP = [None]
KSKIP = set()


FWREF = [None]


def ckpt(name):
    if KSTOP[0] == name:
        FWREF[0].muted = True


def build_program():
    nc = bass.Bass("TRN2", target_bir_lowering=False)

    def din(name, shape):
        return nc.dram_tensor(name, list(shape), F32, kind="ExternalInput").ap()

    def dout(name, shape):
        return nc.dram_tensor(name, list(shape), F32, kind="ExternalOutput").ap()

    xp = din("xp", [SEQ, D])
    xsm = din("xsm", [NS, D])
    ck = din("ck", [NS, 128, 256])
    cv = din("cv", [NS, 128, 256])
    w_in_a = din("w_in_a", [D, 3 * AW])
    w_out_a = din("w_out_a", [AW, D])
    w_kv = din("w_kv", [D, 512])
    w_in_b = din("w_in_b", [D, 2048])
    w_out_b = din("w_out_b", [D, D])
    gaT_d = din("gaT", [128, 8])
    gvT_d = din("gvT", [128, 16])
    gkvT_d = din("gkvT", [128, 8])
    gbT_d = din("gbT", [128, 8])
    gv_row = din("gv_row", [AW])
    gf_row = din("gf_row", [D])
    wsT_d = din("wsT", [128, 8 * 128])
    w00_d = din("w00", [8])
    bs_d = din("bs", [8 * 128])
    bs0_d = din("bs0", [8])
    sinks_d = din("sinks", [16])
    rope_d = din("rope", [SEQ + 128, 16])
    ident_d = din("ident", [128, 128])
    cmask_d = din("cmask", [128, 128])
    mprev_d = din("mprev", [128, 512])
    mcur_d = din("mcur", [128, 512])

    yp = dout("yp", [SEQ, D])
    ysm = dout("ysm", [NS, D])
    nkp = dout("nkp", [128, 256])
    nvp = dout("nvp", [128, 256])
    nks = dout("nks", [NS, 128, 256])
    nvs = dout("nvs", [NS, 128, 256])
    nav = dout("nav", [NS, AW])
    h1s = nc.dram_tensor("h1s", [SEQ + 128, D], F32).ap()

    with contextlib.ExitStack() as es:
        fw = FW(nc, es)
        FWREF[0] = fw
        pe, act, dve, pool, sp = fw.pe, fw.act, fw.dve, fw.pool, fw.sp
        V, S, G, P_ = nc.vector, nc.scalar, nc.gpsimd, nc.tensor

        def rstd_from_ss(ss_ap, tmp_t, out_t, n, width):
            fw.op(dve, lambda: V.tensor_scalar(out=tmp_t[0:n, 0:1], in0=ss_ap, scalar1=1.0 / width, scalar2=EPS,
                                               op0=ALU.mult, op1=ALU.add), reads=[tmp_t.src], writes=[tmp_t])
            fw.op(act, lambda: S.activation(out=tmp_t[0:n, 0:1], in_=tmp_t[0:n, 0:1], func=AF.Ln),
                  reads=[tmp_t], writes=[tmp_t])
            fw.op(act, lambda: S.activation(out=out_t[0:n, 0:1], in_=tmp_t[0:n, 0:1], func=AF.Exp, scale=-0.5),
                  reads=[tmp_t], writes=[out_t])

        def body():
            identb = fw.sb("identb", [128, 128], BF16, dma=True)
            fw.dma(pool, identb[:], ident_d[:, :], identb, writes=[identb])
            gaT = fw.sb("gaT", [128, 8], F32, dma=True)
            fw.dma(sp, gaT[:], gaT_d[:, :], gaT, writes=[gaT])
            gvT = fw.sb("gvT", [128, 16], F32, dma=True)
            fw.dma(sp, gvT[:], gvT_d[:, :], gvT, writes=[gvT])
            gkvT = fw.sb("gkvT", [128, 8], F32, dma=True)
            fw.dma(sp, gkvT[:], gkvT_d[:, :], gkvT, writes=[gkvT])
            gbT = fw.sb("gbT", [128, 8], F32, dma=True)
            fw.dma(sp, gbT[:], gbT_d[:, :], gbT, writes=[gbT])
            onesf = fw.sb("onesf", [128, 8], F32)
            fw.op(pool, lambda: G.memset(onesf[:], 1.0), writes=[onesf])
            ssx = fw.sb("ssx", [128, 1], F32)
            tmx = fw.sb("tmx", [128, 1], F32)
            rsx = fw.sb("rsx", [128, 1], F32)
            tmx.src = ssx
            ssv = fw.sb("ssv", [128, 4], F32)
            ssv1 = fw.sb("ssv1", [128, 1], F32)
            tmv = fw.sb("tmv", [128, 1], F32)
            rsv = fw.sb("rsv", [128, 1], F32)
            tmv.src = ssv1

            WSH = [fw.sb(f"WSH{i}", [128, 8, 512], BF16, dma=True) for i in range(4)]
            with contextlib.ExitStack() as ea:
                WA = [WSH[i - 4] if 4 <= i < 8 else fw.sb(f"WA{i}", [128, 8, 512], BF16, dma=True, es=ea) for i in range(12)]
                WO = [fw.sb(f"WO{i}", [128, 16, 512], BF16, dma=True, es=ea) for i in range(2)]
                wsTm = fw.sb("wsTm", [128, 8, 128], BF16, dma=True, es=ea)
                cmask = fw.sb("cmask", [128, 128], BF16, dma=True, es=ea)
                btile = fw.sb("btile", [128, 8, 128], BF16, es=ea)
                bs0 = fw.sb("bs0", [128, 8], F32, dma=True, es=ea)
                w00 = fw.sb("w00", [128, 8], F32, dma=True, es=ea)
                xs = [fw.sb(f"xs{i}", [128, D], F32, dma=True, es=ea) for i in range(2)]
                hs = [fw.sb(f"hsA{i}", [128, D], F32, dma=True, es=ea) for i in range(2)]
                xsn2 = [fw.sb(f"xsn{i}", [128, D], BF16, es=ea) for i in range(2)]
                xT = fw.sb("xT", [128, 8, 512], BF16, es=ea)
                vraw = [fw.sb("vraw0", [128, AW], BF16, es=ea)]
                wss = [fw.sb("wss0", [128, 8, 128], BF16, es=ea)]
                yT = fw.sb("yT", [128, 16, 512], BF16, es=ea)
                sgb = [fw.sb(f"sgb{i}", [128, 512], BF16, es=ea) for i in range(2)]
                usg = [fw.sb(f"usg{i}", [128, 512], BF16, es=ea) for i in range(2)]
                zzb = [fw.sb(f"zzb{i}", [128, 512], BF16, es=ea) for i in range(2)]
                pa = [fw.ps(f"pa{i}", [128, 512], F32, es=ea) for i in range(8)]
                pav = {id(t_): TV(t_, t_.ap[:].bitcast(BF16).rearrange("p (k t) -> p k t", k=8)) for t_ in pa}
                pac = [0]

                def next_pa():
                    pac[0] += 1
                    return pa[pac[0] % 8]
                eh = ea.enter_context(contextlib.ExitStack())
                vraw += [fw.sb(f"vraw{i}", [128, AW], BF16, es=eh) for i in range(1, 4)]
                wss += [fw.sb(f"wss{i}", [128, 8, 128], BF16, es=eh) for i in range(1, 4)]

                fw.dma(pool, cmask[:], cmask_d[:, :], cmask, writes=[cmask])
                fw.dma(pool, wsTm[:].rearrange("p g i -> p (g i)"), wsT_d[:, :], wsTm, writes=[wsTm])
                fw.dma(sp, xs[0][:], bs_d.partition_broadcast(128), xs[0], writes=[xs[0]])
                fw.op(act, lambda: S.activation(out=btile[:].rearrange("p g i -> p (g i)"), in_=xs[0][:], func=AF.Copy),
                      reads=[xs[0]], writes=[btile])
                fw.dma(sp, bs0[:], bs0_d.partition_broadcast(128), bs0, writes=[bs0])
                fw.dma(sp, w00[:], w00_d.partition_broadcast(128), w00, writes=[w00])
                w_in_v = w_in_a.rearrange("(k p) c -> p k c", p=128)
                for i in [4, 5, 6, 7, 0, 8, 1, 9, 2, 10, 3, 11]:
                    fw.dma(pool, WA[i][:], w_in_v[:, :, i * 512:(i + 1) * 512], WA[i], writes=[WA[i]])
                w_out_v = w_out_a.rearrange("(k p) c -> p k c", p=128)
                for i in range(2):
                    fw.dma(pool, WO[i][:], w_out_v[:, :, i * 512:(i + 1) * 512], WO[i], writes=[WO[i]])
                fw.op(dve, lambda: V.tensor_tensor(out=wsTm[:], in0=wsTm[:],
                                                   in1=cmask[:].unsqueeze(1).broadcast_to([128, 8, 128]), op=ALU.mult),
                      reads=[wsTm, cmask], writes=[wsTm])
                ckpt('consts')
                wd = None

                def front_stats(x_rows, c, nt, sample):
                    xb = xs[c % 2]
                    xsn = xsn2[c % 2]
                    nld = NS if sample else nt
                    fw.dma(sp, xb[0:nld, :], x_rows(c), xb, writes=[xb])
                    fw.op(act, lambda: S.activation(out=xsn[0:nt, :], in_=xb[0:nt, :], func=AF.Square,
                                                    accum_out=ssx[0:nt, :]), reads=[xb], writes=[xsn, ssx])
                    rstd_from_ss(ssx[0:nt, 0:1], tmx, rsx, nt, D)
                    fw.op(act, lambda: S.activation(out=xsn[0:nt, :], in_=xb[0:nt, :], func=AF.Copy,
                                                    scale=rsx[0:nt, 0:1]), reads=[xb, rsx], writes=[xsn])

                xTc = [T(f"xTc{c}", xT.ap[:, :, c * 128:(c + 1) * 128]) for c in range(4)]

                def front_T(c, nt):
                    xsn = xsn2[c % 2]
                    pT = pav[id(next_pa())]
                    fw.group([lambda k=k: P_.transpose(pT[:, k, 0:nt], xsn[0:nt, k * 128:(k + 1) * 128],
                                                       identb[0:nt, 0:nt]) for k in range(8)],
                             reads=[xsn, identb], writes=[pT])
                    fw.op(dve, lambda: V.tensor_tensor(out=xTc[c][:, :, 0:nt], in0=pT[:, :, 0:nt],
                                                       in1=gaT[:].unsqueeze(2).broadcast_to([128, 8, nt]), op=ALU.mult),
                          reads=[pT, gaT], writes=[xTc[c]])

                def block_prologue(x_rows, nch, nt, sample):
                    front_stats(x_rows, 0, nt, sample)
                    if nch > 1:
                        front_stats(x_rows, 1, nt, sample)
                    front_T(0, nt)

                def layer_a_block(x_rows, h_rows, nch, nt, sample, pre_done=False, next_front=None):
                    N = (nch - 1) * 128 + nt
                    if not pre_done:
                        block_prologue(x_rows, nch, nt, sample)
                    for c in range(nch):
                        if c + 2 < nch:
                            front_stats(x_rows, c + 2, nt, sample)
                        if c + 1 < nch:
                            front_T(c + 1, nt)
                        ckpt('s_xT' if sample else 'xT')
                        for cb in range(4):
                            pvb = next_pa()
                            wt = WA[4 + cb]
                            fw.group([lambda k=k: P_.matmul(pvb[0:nt, :], xTc[c][:, k, 0:nt], wt[:, k, :],
                                                            start=(k == 0), stop=(k == 7)) for k in range(8)],
                                     reads=[xTc[c], wt], writes=[pvb])
                            fw.op(act, lambda: S.activation(out=vraw[c][0:nt, cb * 512:(cb + 1) * 512], in_=pvb[0:nt, :],
                                                            func=AF.Square, accum_out=ssv[0:nt, cb:cb + 1]),
                                  reads=[pvb], writes=[vraw[c], ssv])
                            fw.op(act, lambda: S.activation(out=vraw[c][0:nt, cb * 512:(cb + 1) * 512], in_=pvb[0:nt, :],
                                                            func=AF.Copy), reads=[pvb], writes=[vraw[c]])
                            if sample and 'vf32' not in KSKIP:
                                fw.op(act, lambda: S.activation(out=vf32[0:nt, cb * 512:(cb + 1) * 512], in_=pvb[0:nt, :],
                                                                func=AF.Copy), reads=[pvb], writes=[vf32])
                        fw.op(dve, lambda: V.reduce_sum(out=ssv1[0:nt, :], in_=ssv[0:nt, :], axis=AX.X),
                              reads=[ssv], writes=[ssv1])
                        rstd_from_ss(ssv1[0:nt, 0:1], tmv, rsv, nt, AW)
                        wsrc = wd if sample else wsTm
                        fw.op(dve, lambda: V.tensor_scalar(out=wss[c][0:nt, :, 0:nt], in0=wsrc[0:nt, :, 0:nt],
                                                           scalar1=rsv[0:nt, 0:1], scalar2=None, op0=ALU.mult),
                              reads=[wsrc, rsv], writes=[wss[c]])
                        ckpt('s_vproj' if sample else 'vproj')
                        if sample and 'nav' not in KSKIP:
                            fw.op(dve, lambda: V.scalar_tensor_tensor(out=vf32[0:nt, :], in0=vf32[0:nt, :], scalar=rsv[0:nt, 0:1],
                                                                      in1=gvrow[0:nt, :], op0=ALU.mult, op1=ALU.mult),
                                  reads=[vf32, rsv, gvrow], writes=[vf32])
                            fw.dma(sp, nav[:, :], vf32[0:NS, :], vf32, reads=[vf32])
                    if sample:
                        fw.dma(pool, WSH[0][:], w_kv.rearrange("(k p) c -> p k c", p=128), WSH[0], writes=[WSH[0]])
                        w_inb_v0 = w_in_b.rearrange("(k p) c -> p k c", p=128)
                        for i_ in range(3):
                            fw.dma(pool, WSH[1 + i_][:], w_inb_v0[:, :, i_ * 512:(i_ + 1) * 512], WSH[1 + i_],
                                   writes=[WSH[1 + i_]])
                    for cc in range(16):
                        g = cc // 2
                        wu = WA[cc // 4]
                        wg = WA[8 + cc // 4]
                        co = (cc % 4) * 128
                        pg, pu, pz = next_pa(), next_pa(), next_pa()
                        fw.group([lambda k=k: P_.matmul(pg[:, 0:N], wg[:, k, co:co + 128], xT[:, k, 0:N],
                                                        start=(k == 0), stop=(k == 7)) for k in range(8)],
                                 reads=[wg] + xTc[:nch], writes=[pg])
                        fw.group([lambda k=k: P_.matmul(pu[:, 0:N], wu[:, k, co:co + 128], xT[:, k, 0:N],
                                                        start=(k == 0), stop=(k == 7)) for k in range(8)],
                                 reads=[wu] + xTc[:nch], writes=[pu])
                        fw.group([lambda c=c: P_.matmul(pz[:, c * 128:c * 128 + nt], vraw[c][0:nt, cc * 128:(cc + 1) * 128],
                                                        wss[c][0:nt, g, 0:nt], start=True, stop=True) for c in range(nch)],
                                 reads=list(vraw[:nch]) + list(wss[:nch]), writes=[pz])
                        sg_, us_, zz_ = sgb[cc % 2], usg[cc % 2], zzb[cc % 2]
                        fw.op(act, lambda: S.activation(out=sg_[:, 0:N], in_=pg[:, 0:N], func=AF.Silu),
                              reads=[pg], writes=[sg_])
                        fw.op(dve, lambda: V.tensor_tensor(out=us_[:, 0:N], in0=pu[:, 0:N], in1=sg_[:, 0:N], op=ALU.mult),
                              reads=[pu, sg_], writes=[us_])
                        if sample:
                            fw.op(dve, lambda: V.scalar_tensor_tensor(out=zz_[:, 0:N], in0=pz[:, 0:N], scalar=gvT[:, cc:cc + 1],
                                                                      in1=bs0[:, g:g + 1].broadcast_to([128, N]),
                                                                      op0=ALU.mult, op1=ALU.add),
                                  reads=[pz, gvT, bs0], writes=[zz_])
                        else:
                            fw.op(dve, lambda: V.scalar_tensor_tensor(
                                out=zz_[:, 0:N].rearrange("p (c i) -> p c i", c=nch),
                                in0=pz[:, 0:N].rearrange("p (c i) -> p c i", c=nch), scalar=gvT[:, cc:cc + 1],
                                in1=btile[:, g, :].unsqueeze(1).broadcast_to([128, nch, 128]),
                                op0=ALU.mult, op1=ALU.add), reads=[pz, gvT, btile], writes=[zz_])
                        fw.op(pool, lambda: G.tensor_tensor(out=yT[:, cc, 0:N], in0=zz_[:, 0:N], in1=us_[:, 0:N], op=ALU.mult),
                              reads=[zz_, us_], writes=[yT])
                        ckpt('s_cc0' if sample else 'cc0')
                    if next_front is not None:
                        next_front()
                    for c in range(nch):
                        hb = hs[c % 2]
                        fw.dma(sp, hb[0:(NS if sample else nt), :], x_rows(c), hb, writes=[hb])
                        for db in range(2):
                            pob = next_pa()
                            fw.group([lambda cc=cc: P_.matmul(pob[0:nt, :], yT[:, cc, c * 128:c * 128 + nt], WO[db][:, cc, :],
                                                              start=(cc == 0), stop=(cc == 15)) for cc in range(16)],
                                     reads=[yT, WO[db]], writes=[pob])
                            fw.op(dve, lambda: V.tensor_tensor(out=hb[0:nt, db * 512:(db + 1) * 512], in0=pob[0:nt, :],
                                                               in1=hb[0:nt, db * 512:(db + 1) * 512], op=ALU.add),
                                  reads=[pob, hb], writes=[hb])
                        fw.dma(pool, h_rows(c), hb[0:nt, :], hb, reads=[hb])
                        ckpt('s_chunk0' if sample else 'chunk0')

                def xrows(blk):
                    return lambda c: xp[(blk * 4 + c) * 128:(blk * 4 + c + 1) * 128, :]

                for blk in range(4):
                    nf = (lambda blk=blk: block_prologue(xrows(blk + 1), 4, 128, False)) if blk < 3 else None
                    layer_a_block(xrows(blk), lambda c, blk=blk: h1s[(blk * 4 + c) * 128:(blk * 4 + c + 1) * 128, :],
                                  4, 128, False, pre_done=(blk > 0), next_front=nf)
                ckpt('blockA')
                fw.barrier_all()
                eh.close()
                vf32 = fw.sb("vf32", [128, AW], F32, dma=True, es=ea)
                gvrow = fw.sb("gvrow", [128, AW], F32, dma=True, es=ea)
                wd = fw.sb("wd", [128, 8, 128], BF16, es=ea)
                fw.dma(sp, gvrow[:], gv_row.partition_broadcast(128), gvrow, writes=[gvrow])
                fw.op(dve, lambda: V.tensor_tensor(out=wd[:], in0=identb[:].unsqueeze(1).broadcast_to([128, 8, 128]),
                                                   in1=w00[:].unsqueeze(2).broadcast_to([128, 8, 128]), op=ALU.mult),
                      reads=[identb, w00], writes=[wd])
                for tz in (xs[0], hs[0]):
                    fw.op(pool, lambda: G.memset(tz[:], 0.0), writes=[tz])
                layer_a_block(lambda c: xsm[:, :], lambda c: h1s[SEQ:SEQ + 128, :], 1, 128, True)
                ckpt('sampleA')
                fw.barrier_all()

            with contextlib.ExitStack() as eb:
                WKV = WSH[0]
                WQG = [WSH[1], WSH[2], WSH[3], fw.sb("WQG3", [128, 8, 512], BF16, dma=True, es=eb)]
                WOB = [fw.sb(f"WOB{i}", [128, 8, 512], BF16, dma=True, es=eb) for i in range(2)]
                w_inb_v = w_in_b.rearrange("(k p) c -> p k c", p=128)
                fw.dma(pool, WQG[3][:], w_inb_v[:, :, 3 * 512:4 * 512], WQG[3], writes=[WQG[3]])
                w_outb_v = w_out_b.rearrange("(k p) c -> p k c", p=128)
                for i in range(2):
                    fw.dma(pool, WOB[i][:], w_outb_v[:, :, i * 512:(i + 1) * 512], WOB[i], writes=[WOB[i]])
                mprev = fw.sb("mprev", [128, 512], BF16, dma=True, es=eb)
                mcur = fw.sb("mcur", [128, 512], BF16, dma=True, es=eb)
                fw.dma(pool, mprev[:], mprev_d[:, :], mprev, writes=[mprev])
                fw.dma(pool, mcur[:], mcur_d[:, :], mcur, writes=[mcur])
                gfrow = fw.sb("gfrow", [128, D], F32, dma=True, es=eb)
                fw.dma(sp, gfrow[:], gf_row.partition_broadcast(128), gfrow, writes=[gfrow])
                esink = fw.sb("esink", [128, 16], F32, dma=True, es=eb)
                fw.dma(sp, esink[:], sinks_d.partition_broadcast(128), esink, writes=[esink])
                fw.op(act, lambda: S.activation(out=esink[:], in_=esink[:], func=AF.Exp), reads=[esink], writes=[esink])
                onesb = fw.sb("onesb", [128, 128], BF16, es=eb)
                fw.op(pool, lambda: G.memset(onesb[:], 1.0), writes=[onesb])

                hs = [fw.sb(f"hsB{i}", [128, D], F32, dma=True, es=eb) for i in range(4)]
                ys = [fw.sb(f"ysB{i}", [128, D], F32, dma=True, es=eb) for i in range(2)]
                rope = [fw.sb(f"rope{i}", [128, 16], F32, dma=True, es=eb) for i in range(2)]
                hsn = fw.sb("hsn", [128, D], BF16, es=eb)
                hkT = fw.sb("hkT", [128, 8, 128], BF16, es=eb)
                hbT = fw.sb("hbT", [128, 8, 128], BF16, es=eb)
                kr = [fw.sb(f"kr{i}", [128, 4, 64], F32, dma=True, es=eb) for i in range(2)]
                vf = [fw.sb(f"vf{i}", [128, 4, 64], F32, dma=True, es=eb) for i in range(2)]
                ta = fw.sb("ta", [128, 16, 8], F32, es=eb)
                tb = fw.sb("tb", [128, 16, 8], F32, es=eb)
                tc = fw.sb("tc", [128, 16, 8], F32, es=eb)
                td = fw.sb("td", [128, 16, 8], F32, es=eb)
                xf16 = fw.sb("xf16", [128, 8, 16], F32, es=eb)
                egb = [fw.sb(f"egb{i}", [128, 512], F32, es=eb) for i in range(2)]
                xq16 = [fw.sb(f"xq16_{i}", [128, 8, 16], F32, es=eb) for i in range(2)]
                kdup = fw.sb("kdup", [128, 4, 2, 64], BF16, es=eb)
                kTz = [fw.sb(f"kTz{i}", [128, 4, 2, 128], BF16, es=eb) for i in range(3)]
                vaug = [fw.sb(f"vaug{i}", [128, 4, 65], BF16, es=eb) for i in range(3)]
                qr = fw.sb("qr", [128, 16, 64], BF16, es=eb)
                sgt2 = [fw.sb(f"sgt{i}", [128, D], BF16, es=eb) for i in range(2)]
                qT2 = [fw.sb(f"qT{i}", [128, 8, 128], BF16, es=eb) for i in range(2)]
                PT = [fw.sb(f"PT{i}", [128, 512], BF16, es=eb) for i in range(4)]
                den = fw.sb("den", [128, 16], F32, es=eb)
                rden = fw.sb("rden", [128, 16], F32, es=eb)
                on = fw.sb("on", [128, 16, 64], BF16, es=eb)
                og = fw.sb("og", [128, D], BF16, es=eb)
                ogT = fw.sb("ogT", [128, 8, 128], BF16, es=eb)
                pp = [fw.ps(f"pp{i}", [128, 512], F32, es=eb) for i in range(5)]
                ppv = {id(t_): TV(t_, t_.ap[:].bitcast(BF16).rearrange("p (k t) -> p k t", k=8)) for t_ in pp}

                def next_pT():
                    return ppv[id(next_pp())]
                po = [fw.ps(f"poB{i}", [128, 7, 72], F32, es=eb) for i in range(3)]
                for i in range(3):
                    fw.op(pool, lambda: G.memset(kTz[i][:], 0.0), writes=[kTz[i]])
                    fw.op(pool, lambda: G.memset(vaug[i][:], 1.0), writes=[vaug[i]])
                ppc = [0]
                kcA = [fw.sb(f"kcA{i}", [128, 4, 4, 2, 64], BF16, dma=True, es=eb) for i in range(4)]
                vcA = [fw.sb(f"vcA{i}", [128, 4, 4, 2, 64], BF16, dma=True, es=eb) for i in range(4)]
                kTA = [fw.sb(f"kTA{i}", [128, 4, 4, 128], BF16, es=eb) for i in range(4)]
                kct, vct = [], []

                def load_cache_group(i):
                    for (grp, src, lst) in ((kcA[i], ck, kct), (vcA[i], cv, vct)):
                        g4 = []
                        for bb in range(4):
                            tk = T(f"{grp.name}_{bb}", grp.ap[:, bb])
                            tk.dsem_sw = fw.holder(grp, pool)
                            fw.dma(pool, tk[:, :, 0, :], src[4 * i + bb].rearrange("s (h d) -> s h d", h=4), tk, writes=[tk])
                            g4.append(tk)
                        for tk in g4:
                            tk.w = (grp.dsem_sw, grp.dsem_sw.cnt)
                        lst.extend(g4)
                ckpt('b_setup')

                def next_pp():
                    ppc[0] += 1
                    return pp[ppc[0] % 5]

                def rotary(xf, nh, dst, nt, rp, writes_t):
                    x1 = xf[0:nt, 0:nh, 0:8]
                    x2 = xf[0:nt, 0:nh, 8:16]
                    cos = rp[0:nt, 0:8].unsqueeze(1).broadcast_to([nt, nh, 8])
                    sin = rp[0:nt, 8:16].unsqueeze(1).broadcast_to([nt, nh, 8])
                    fw.op(dve, lambda: V.tensor_tensor(out=ta[0:nt, 0:nh, :], in0=x1, in1=cos, op=ALU.mult),
                          reads=[xf, rp], writes=[ta])
                    fw.op(dve, lambda: V.tensor_tensor(out=tb[0:nt, 0:nh, :], in0=x2, in1=sin, op=ALU.mult),
                          reads=[xf, rp], writes=[tb])
                    fw.op(dve, lambda: V.tensor_tensor(out=tc[0:nt, 0:nh, :], in0=x2, in1=cos, op=ALU.mult),
                          reads=[xf, rp], writes=[tc])
                    fw.op(dve, lambda: V.tensor_tensor(out=td[0:nt, 0:nh, :], in0=x1, in1=sin, op=ALU.mult),
                          reads=[xf, rp], writes=[td])
                    fw.op(dve, lambda: V.tensor_tensor(out=dst[:, :, 0:8], in0=ta[0:nt, 0:nh, :], in1=tb[0:nt, 0:nh, :],
                                                       op=ALU.subtract), reads=[ta, tb], writes=[writes_t])
                    fw.op(dve, lambda: V.tensor_tensor(out=dst[:, :, 8:16], in0=tc[0:nt, 0:nh, :], in1=td[0:nt, 0:nh, :],
                                                       op=ALU.add), reads=[tc, td], writes=[writes_t])

                def stage_a_load(ti, nt, rows_in, rope_rows):
                    hb = hs[ti % 4]
                    rp = rope[ti % 2]
                    fw.dma(sp, hb[0:nt, :], rows_in, hb, writes=[hb])
                    fw.dma(sp, rp[0:nt, :], rope_rows, rp, writes=[rp])

                def stage_a(ti, nt):
                    hb = hs[ti % 4]
                    fw.op(act, lambda: S.activation(out=hsn[0:nt, :], in_=hb[0:nt, :], func=AF.Square,
                                                    accum_out=ssx[0:nt, :]), reads=[hb], writes=[hsn, ssx])
                    rstd_from_ss(ssx[0:nt, 0:1], tmx, rsx, nt, D)
                    fw.op(act, lambda: S.activation(out=hsn[0:nt, :], in_=hb[0:nt, :], func=AF.Copy,
                                                    scale=rsx[0:nt, 0:1]), reads=[hb, rsx], writes=[hsn])

                def stage_b1(ti, nt):
                    pT = next_pT()
                    fw.group([lambda k=k: P_.transpose(pT[:, k, 0:nt], hsn[0:nt, k * 128:(k + 1) * 128],
                                                       identb[0:nt, 0:nt]) for k in range(8)],
                             reads=[hsn, identb], writes=[pT])
                    fw.op(dve, lambda: V.tensor_tensor(out=hkT[:, :, 0:nt], in0=pT[:, :, 0:nt],
                                                       in1=gkvT[:].unsqueeze(2).broadcast_to([128, 8, nt]), op=ALU.mult),
                          reads=[pT, gkvT], writes=[hkT])
                    fw.op(dve, lambda: V.tensor_tensor(out=hbT[:, :, 0:nt], in0=pT[:, :, 0:nt],
                                                       in1=gbT[:].unsqueeze(2).broadcast_to([128, 8, nt]), op=ALU.mult),
                          reads=[pT, gbT], writes=[hbT])

                def stage_b2(ti, nt, par, sample):
                    sgt = sgt2[ti % 2]
                    par3 = ti % 3
                    rp = rope[ti % 2]
                    pk = next_pp()
                    fw.group([lambda k=k: P_.matmul(pk[0:nt, :], hkT[:, k, 0:nt], WKV[:, k, :], start=(k == 0), stop=(k == 7))
                              for k in range(8)], reads=[hkT, WKV], writes=[pk])
                    pqs = []
                    for cb in range(2 if sample else 4):
                        pq = next_pp() if cb < 2 else None
                        pqs.append(pq)
                    krb, vfb = kr[par], vf[par]
                    fw.op(act, lambda: S.activation(out=krb[0:nt].rearrange("p h d -> p (h d)"), in_=pk[0:nt, 0:256],
                                                    func=AF.Copy), reads=[pk], writes=[krb])
                    fw.op(act, lambda: S.activation(out=xf16[0:nt, 0:4, :],
                                                    in_=pk[0:nt, 0:256].rearrange("p (h d) -> p h d", h=4)[:, :, 0:16],
                                                    func=AF.Copy), reads=[pk], writes=[xf16])
                    fw.op(act, lambda: S.activation(out=vfb[0:nt].rearrange("p h d -> p (h d)"), in_=pk[0:nt, 256:512],
                                                    func=AF.Copy), reads=[pk], writes=[vfb])
                    fw.op(act, lambda: S.activation(out=vaug[par3][0:nt, :, 0:64],
                                                    in_=pk[0:nt, 256:512].rearrange("p (h d) -> p h d", h=4), func=AF.Copy),
                          reads=[pk], writes=[vaug[par3]])
                    rotary(xf16, 4, krb[0:nt], nt, rp, krb)
                    fw.op(pool, lambda: G.tensor_copy(out=kdup[0:nt], in_=krb[0:nt].unsqueeze(2).broadcast_to([nt, 4, 2, 64])),
                          reads=[krb], writes=[kdup])
                    for cb in range(2):
                        pq = pqs[cb]
                        fw.group([lambda k=k: P_.matmul(pq[0:nt, :], hbT[:, k, 0:nt], WQG[cb][:, k, :],
                                                        start=(k == 0), stop=(k == 7)) for k in range(8)],
                                 reads=[hbT, WQG[cb]], writes=[pq])
                        fw.op(act, lambda: S.activation(out=qr[0:nt, cb * 8:(cb + 1) * 8].rearrange("p h d -> p (h d)"),
                                                        in_=pq[0:nt, :], func=AF.Copy), reads=[pq], writes=[qr])
                        xq = xq16[cb]
                        fw.op(act, lambda: S.activation(out=xq[0:nt, :, :],
                                                        in_=pq[0:nt, :].rearrange("p (h d) -> p h d", h=8)[:, :, 0:16],
                                                        func=AF.Copy), reads=[pq], writes=[xq])
                        rotary(xq, 8, qr[0:nt, cb * 8:(cb + 1) * 8], nt, rp, qr)
                    if not sample:
                        for cb in range(2):
                            pq = next_pp()
                            fw.group([lambda k=k: P_.matmul(pq[0:nt, :], hbT[:, k, 0:nt], WQG[2 + cb][:, k, :],
                                                            start=(k == 0), stop=(k == 7)) for k in range(8)],
                                     reads=[hbT, WQG[2 + cb]], writes=[pq])
                            fw.op(act, lambda: S.activation(out=sgt[0:nt, cb * 512:(cb + 1) * 512], in_=pq[0:nt, :],
                                                            func=AF.Silu), reads=[pq], writes=[sgt])

                def stage_b3(ti, nt):
                    qT = qT2[ti % 2]
                    par3 = ti % 3
                    pT = next_pT()
                    fw.group([lambda kh=kh: P_.transpose(pT[:, kh, 0:nt], kdup[0:nt, kh].rearrange("p r d -> p (r d)"),
                                                         identb[0:nt, 0:nt]) for kh in range(4)],
                             reads=[kdup, identb], writes=[pT])
                    fw.op(act, lambda: S.activation(out=kTz[par3][0:64, :, 0, 0:nt], in_=pT[0:64, 0:4, 0:nt], func=AF.Copy),
                          reads=[pT], writes=[kTz[par3]])
                    fw.op(act, lambda: S.activation(out=kTz[par3][64:128, :, 1, 0:nt], in_=pT[64:128, 0:4, 0:nt], func=AF.Copy),
                          reads=[pT], writes=[kTz[par3]])
                    pT = next_pT()
                    fw.group([lambda m=m: P_.transpose(pT[:, m, 0:nt], qr[0:nt, 2 * m:2 * m + 2].rearrange("p h d -> p (h d)"),
                                                       identb[0:nt, 0:nt]) for m in range(8)],
                             reads=[qr, identb], writes=[pT])
                    fw.op(dve, lambda: V.tensor_tensor(out=qT[:, :, 0:nt], in0=pT[:, :, 0:nt],
                                                       in1=onesf[:].unsqueeze(2).broadcast_to([128, 8, nt]), op=ALU.mult),
                          reads=[pT, onesf], writes=[qT])

                def layer_b_tail(ti, nt, hb, rows_out, n_out=128):
                    for db in range(2):
                        pob = next_pp()
                        fw.group([lambda m=m: P_.matmul(pob[0:nt, :], ogT[:, m, 0:nt], WOB[db][:, m, :],
                                                        start=(m == 0), stop=(m == 7)) for m in range(8)],
                                 reads=[ogT, WOB[db]], writes=[pob])
                        fw.op(dve, lambda: V.tensor_tensor(out=hb[0:nt, db * 512:(db + 1) * 512], in0=pob[0:nt, :],
                                                           in1=hb[0:nt, db * 512:(db + 1) * 512], op=ALU.add),
                              reads=[pob, hb], writes=[hb])
                    yb = ys[ti % 2]
                    fw.op(act, lambda: S.activation(out=yb[0:nt, :], in_=hb[0:nt, :], func=AF.Square,
                                                    accum_out=ssv1[0:nt, :]), reads=[hb], writes=[yb, ssv1])
                    rstd_from_ss(ssv1[0:nt, 0:1], tmv, rsv, nt, D)
                    fw.op(dve, lambda: V.scalar_tensor_tensor(out=yb[0:nt, :], in0=hb[0:nt, :], scalar=rsv[0:nt, 0:1],
                                                              in1=gfrow[0:nt, :], op0=ALU.mult, op1=ALU.mult),
                          reads=[hb, rsv, gfrow], writes=[yb])
                    fw.dma(pool, rows_out, yb[0:n_out, :], yb, reads=[yb])

                def attn_scores(ti, kh):
                    qT = qT2[ti % 2]
                    kbs = ([((ti - 1) % 3, mprev)] if ti > 0 else []) + [(ti % 3, mcur)]
                    pts = []
                    for kbi, (kpar, msk) in enumerate(kbs):
                        psb = next_pp()
                        fw.group([
                            lambda: P_.matmul(psb[:, :], identb[:], msk[:], start=True, stop=False),
                            lambda: P_.matmul(psb[:, 0:256], kTz[kpar][:, kh, 0, :],
                                              qT[:, 2 * kh:2 * kh + 2, :].rearrange("p m q -> p (m q)"),
                                              start=False, stop=False),
                            lambda: P_.matmul(psb[:, 256:512], kTz[kpar][:, kh, 1, :],
                                              qT[:, 2 * kh:2 * kh + 2, :].rearrange("p m q -> p (m q)"),
                                              start=False, stop=True),
                        ], reads=[identb, msk, kTz[kpar], qT], writes=[psb])
                        ptb = PT[(kh % 2) * 2 + kbi]
                        fw.op(act, lambda: S.activation(out=ptb[:], in_=psb[:], func=AF.Exp, scale=0.125),
                              reads=[psb], writes=[ptb])
                        pts.append((ptb, kpar))
                    return pts

                def attn_pv(ti, kh, pts):
                    for r in range(2):
                        for mm in range(2):
                            h = 4 * kh + 2 * mm + r
                            bank, slot = po[h // 7], h % 7
                            c0 = r * 256 + mm * 128
                            fw.group([lambda i=i: P_.matmul(bank[:, slot, 0:65], pts[i][0][:, c0:c0 + 128],
                                                            vaug[pts[i][1]][:, kh, :], start=(i == 0),
                                                            stop=(i == len(pts) - 1)) for i in range(len(pts))],
                                     reads=[p[0] for p in pts] + [vaug[p[1]] for p in pts], writes=[bank])

                def attn_finish(ti):
                    sgt = sgt2[ti % 2]
                    for b in range(3):
                        h0 = 7 * b
                        nh = min(7, 16 - h0)
                        fw.op(dve, lambda: V.tensor_tensor(out=den[:, h0:h0 + nh].unsqueeze(2), in0=po[b][:, 0:nh, 64:65],
                                                           in1=esink[:, h0:h0 + nh].unsqueeze(2), op=ALU.add),
                              reads=[po[b], esink], writes=[den])
                    fw.op(dve, lambda: V.reciprocal(out=rden[:], in_=den[:]), reads=[den], writes=[rden])
                    for b in range(3):
                        h0 = 7 * b
                        nh = min(7, 16 - h0)
                        fw.op(dve, lambda: V.tensor_tensor(out=on[:, h0:h0 + nh, :], in0=po[b][:, 0:nh, 0:64],
                                                           in1=rden[:, h0:h0 + nh].unsqueeze(2).broadcast_to([128, nh, 64]),
                                                           op=ALU.mult), reads=[po[b], rden], writes=[on])
                    fw.op(pool, lambda: G.tensor_tensor(out=og[:], in0=on[:].rearrange("p h d -> p (h d)"), in1=sgt[:],
                                                        op=ALU.mult), reads=[on, sgt], writes=[og])
                    pT = next_pT()
                    fw.group([lambda m=m: P_.transpose(pT[:, m, :], og[:, m * 128:(m + 1) * 128], identb[:]) for m in range(8)],
                             reads=[og, identb], writes=[pT])
                    fw.op(act, lambda: S.activation(out=ogT[:], in_=pT[:], func=AF.Copy), reads=[pT], writes=[ogT])

                def rows(ti):
                    if ti == NT:
                        return h1s[SEQ:SEQ + 128, :], rope_d[SEQ:SEQ + 128, :]
                    return h1s[ti * 128:(ti + 1) * 128, :], rope_d[ti * 128:(ti + 1) * 128, :]

                def attn_og(ti):
                    sgt = sgt2[ti % 2]
                    for b in range(3):
                        h0 = 7 * b
                        nh = min(7, 16 - h0)
                        fw.op(dve, lambda: V.tensor_tensor(out=den[:, h0:h0 + nh].unsqueeze(2), in0=po[b][:, 0:nh, 64:65],
                                                           in1=esink[:, h0:h0 + nh].unsqueeze(2), op=ALU.add),
                              reads=[po[b], esink], writes=[den])
                    fw.op(dve, lambda: V.reciprocal(out=rden[:], in_=den[:]), reads=[den], writes=[rden])
                    for b in range(3):
                        h0 = 7 * b
                        nh = min(7, 16 - h0)
                        fw.op(dve, lambda: V.tensor_tensor(out=on[:, h0:h0 + nh, :], in0=po[b][:, 0:nh, 0:64],
                                                           in1=rden[:, h0:h0 + nh].unsqueeze(2).broadcast_to([128, nh, 64]),
                                                           op=ALU.mult), reads=[po[b], rden], writes=[on])
                    fw.op(pool, lambda: G.tensor_tensor(out=og[:], in0=on[:].rearrange("p h d -> p (h d)"), in1=sgt[:],
                                                        op=ALU.mult), reads=[on, sgt], writes=[og])

                def og_transpose():
                    pT = next_pT()
                    fw.group([lambda m=m: P_.transpose(pT[:, m, :], og[:, m * 128:(m + 1) * 128], identb[:]) for m in range(8)],
                             reads=[og, identb], writes=[pT])
                    fw.op(act, lambda: S.activation(out=ogT[:], in_=pT[:], func=AF.Copy), reads=[pT], writes=[ogT])

                stage_a_load(0, 128, *rows(0))
                stage_a(0, 128)
                for i in range(-1, NT + 1):
                    nx = i + 1
                    have_nx = nx <= NT
                    cur = 0 <= i < NT
                    if i + 2 <= NT:
                        stage_a_load(i + 2, 128, *rows(i + 2))
                    if 0 <= i < 4:
                        load_cache_group(i)
                    if 6 <= i < 14:
                        gi = (i - 6) // 2
                        ca = (kcA if i % 2 == 0 else vcA)[gi]
                        cts = (kct if i % 2 == 0 else vct)[4 * gi:4 * gi + 4]
                        fw.op(pool, lambda: G.tensor_copy(out=ca[:, :, :, 1, :], in_=ca[:, :, :, 0, :]), reads=cts, writes=cts)
                    if 7 <= i <= 14:
                        for b_ in (2 * (i - 7), 2 * (i - 7) + 1):
                            gi, bb = b_ // 4, b_ % 4
                            kcb = kct[b_]
                            pT = next_pT()
                            fw.group([lambda kh=kh: P_.transpose(pT[:, kh, :], kcb[:, kh].rearrange("p r d -> p (r d)"),
                                                                 identb[:]) for kh in range(4)], reads=[kcb, identb], writes=[pT])
                            if bb % 2 == 0:
                                fw.op(act, lambda: S.activation(out=kTA[gi][:, bb], in_=pT[:, 0:4, :], func=AF.Copy),
                                      reads=[pT], writes=[kTA[gi]])
                            else:
                                fw.op(dve, lambda: V.tensor_tensor(out=kTA[gi][:, bb], in0=pT[:, 0:4, :],
                                                                   in1=onesf[:, 0:4].unsqueeze(2).broadcast_to([128, 4, 128]),
                                                                   op=ALU.mult), reads=[pT, onesf], writes=[kTA[gi]])
                    if have_nx:
                        stage_b1(nx, 128)
                    if cur:
                        p0 = attn_scores(i, 0)
                        p1 = attn_scores(i, 1)
                    if have_nx:
                        stage_b2(nx, 128, nx % 2, nx == NT)
                        if nx == NT - 1:
                            kp = nx % 2
                            fw.dma(sp, nkp[:, :], kr[kp][:].rearrange("p h d -> p (h d)"), kr[kp], reads=[kr[kp]])
                            fw.dma(sp, nvp[:, :], vf[kp][:].rearrange("p h d -> p (h d)"), vf[kp], reads=[vf[kp]])
                    if i >= 1:
                        og_transpose()
                    if cur:
                        attn_pv(i, 0, p0)
                        p2 = attn_scores(i, 2)
                        attn_pv(i, 1, p1)
                        p3 = attn_scores(i, 3)
                    if i >= 1:
                        layer_b_tail(i - 1, 128, hs[(i - 1) % 4], yp[(i - 1) * 128:i * 128, :])
                    if nx + 1 <= NT:
                        stage_a(nx + 1, 128)
                    if cur:
                        attn_pv(i, 2, p2)
                        attn_pv(i, 3, p3)
                    if have_nx:
                        stage_b3(nx, 128)
                    if cur:
                        attn_og(i)

                ckpt('b_prompt')
                with contextlib.ExitStack() as e2:
                    PTs = fw.sb("PTs", [128, NS, 16], BF16, es=e2)
                    sgT = fw.sb("sgT", [128, 8, NS], BF16, es=e2)
                    prod = fw.sb("prod", [128, 16, 64], F32, es=e2)
                    snew = fw.sb("snew", [128, 16], F32, es=e2)
                    pm = fw.sb("pm", [128, NS, 16], BF16, es=e2)
                    vnd = fw.sb("vnd", [128, 4, 2, 64], BF16, es=e2)
                    onew = fw.sb("onew", [128, NS, 16], F32, es=e2)
                    dens = fw.sb("dens", [128, NS, 16], F32, es=e2)
                    osb = fw.sb("osb", [128, NS, 16], F32, es=e2)
                    cpy = T("cpy")
                    fw.add_dsem(cpy)
                    par = 0
                    hb = hs[NT % 4]
                    qT = qT2[NT % 2]
                    krb, vfb = kr[par], vf[par]
                    fw.dma(sp, nks[:, 0:127, :], ck[:, 1:128, :], cpy)
                    fw.dma(sp, nvs[:, 0:127, :], cv[:, 1:128, :], cpy)
                    fw.dma(sp, nks[:, 127, :], krb[0:NS].rearrange("p h d -> p (h d)"), kr[par], reads=[krb])
                    fw.dma(sp, nvs[:, 127, :], vfb[0:NS].rearrange("p h d -> p (h d)"), vf[par], reads=[vfb])
                    pgs = next_pp()
                    for m in range(8):
                        wq = WQG[2 + m // 4]
                        co = (m % 4) * 128
                        fw.group([lambda k=k: P_.matmul(pgs[:, m * NS:(m + 1) * NS], wq[:, k, co:co + 128], hbT[:, k, 0:NS],
                                                        start=(k == 0), stop=(k == 7)) for k in range(8)],
                                 reads=[wq, hbT], writes=[pgs])
                    fw.op(act, lambda: S.activation(out=sgT[:].rearrange("p m b -> p (m b)"), in_=pgs[:, 0:8 * NS],
                                                    func=AF.Silu), reads=[pgs], writes=[sgT])
                    ckpt('s_front')
                    qz = fw.sb("qz", [128, 2, 8, NS], BF16, es=e2)
                    fw.op(pool, lambda: G.memset(qz[:], 0.0), writes=[qz])
                    for r_ in range(2):
                        rw = slice(64 * r_, 64 * r_ + 64)
                        fw.op(dve, lambda: V.tensor_copy(out=qz[rw, r_, :, :], in_=qT[rw, :, 0:NS]), reads=[qT], writes=[qz])
                    pSs = next_pp()
                    for b in range(NS):
                        kta = kTA[b // 4]
                        fns = []
                        for kh in range(4):
                            for r in range(2):
                                fns.append(lambda kh=kh, r=r: P_.matmul(
                                    pSs[:, b * 16 + 4 * kh + r:b * 16 + 4 * kh + r + 3:2], kta[:, b % 4, kh, :],
                                    qz[:, r, 2 * kh:2 * kh + 2, b], start=True, stop=True))
                        fw.group(fns, reads=[kta, qz], writes=[pSs])
                    fw.op(act, lambda: S.activation(out=PTs[:].rearrange("p b h -> p (b h)"), in_=pSs[:, 0:NS * 16],
                                                    func=AF.Exp, scale=0.125), reads=[pSs], writes=[PTs])
                    ckpt('s_scores')
                    pO = next_pp()
                    for b in range(NS):
                        vcb = vct[b]
                        fw.group([lambda kh=kh: P_.matmul(pO[:, b * 16 + 4 * kh:b * 16 + 4 * kh + 4],
                                                          vcb[:, kh].rearrange("p r d -> p (r d)"),
                                                          PTs[:, b, 4 * kh:4 * kh + 4], start=True, stop=True)
                                  for kh in range(4)], reads=[vcb, PTs], writes=[pO])
                    pD = next_pp()
                    fw.group([lambda: P_.matmul(pD[:, 0:NS * 16], onesb[:], PTs[:].rearrange("p b h -> p (b h)"),
                                                start=True, stop=True)], reads=[onesb, PTs], writes=[pD])
                    fw.op(act, lambda: S.activation(out=osb[:].rearrange("p b h -> p (b h)"), in_=pO[:, 0:NS * 16],
                                                    func=AF.Copy), reads=[pO], writes=[osb])
                    fw.op(act, lambda: S.activation(out=dens[:].rearrange("p b h -> p (b h)"), in_=pD[:, 0:NS * 16],
                                                    func=AF.Copy), reads=[pD], writes=[dens])
                    ckpt('s_pv')
                    fw.op(dve, lambda: V.tensor_tensor(out=prod[:].rearrange("p (k g) d -> p k g d", k=4),
                                                       in0=qr[:].rearrange("p (k g) d -> p k g d", k=4),
                                                       in1=krb[:].unsqueeze(2).broadcast_to([128, 4, 4, 64]), op=ALU.mult),
                          reads=[qr, krb], writes=[prod])
                    fw.op(dve, lambda: V.reduce_sum(out=snew[:], in_=prod[:], axis=AX.X), reads=[prod], writes=[snew])
                    fw.op(act, lambda: S.activation(out=snew[:], in_=snew[:], func=AF.Exp, scale=0.125),
                          reads=[snew], writes=[snew])
                    pm4 = pm[:].rearrange("p b h -> p (b h)").rearrange("p (k b g) -> p k b g", k=4, b=NS)
                    fw.op(dve, lambda: V.tensor_tensor(
                        out=pm4, in0=identb[:, 0:NS].unsqueeze(1).unsqueeze(3).broadcast_to([128, 4, NS, 4]),
                        in1=snew[:].rearrange("p (k g) -> p k g", k=4).unsqueeze(2).broadcast_to([128, 4, NS, 4]),
                        op=ALU.mult), reads=[identb, snew], writes=[pm])
                    fw.op(dve, lambda: V.tensor_copy(out=vnd[:], in_=vfb[:].unsqueeze(2).broadcast_to([128, 4, 2, 64])),
                          reads=[vfb], writes=[vnd])
                    pN = next_pp()
                    pmf = pm[:].rearrange("p b h -> p (b h)")
                    fw.group([lambda kh=kh: P_.matmul(
                        pN[:, kh * 64:(kh + 1) * 64], vnd[:, kh].rearrange("p r d -> p (r d)"),
                        pmf[:, kh * 64:(kh + 1) * 64], start=True, stop=True)
                        for kh in range(4)], reads=[vnd, pm], writes=[pN])
                    osb4 = osb[:].rearrange("p b (k g) -> p b k g", k=4)
                    fw.op(dve, lambda: V.tensor_tensor(
                        out=osb4, in0=pN[:, 0:NS * 16].rearrange("p (k b g) -> p b k g", k=4, b=NS), in1=osb4,
                        op=ALU.add), reads=[pN, osb], writes=[osb])
                    pN2 = next_pp()
                    fw.group([lambda: P_.matmul(pN2[:, 0:NS * 16], onesb[:], pm[:].rearrange("p b h -> p (b h)"),
                                                start=True, stop=True)], reads=[onesb, pm], writes=[pN2])
                    dens4 = dens[:].rearrange("p b (k g) -> p b k g", k=4)
                    fw.op(dve, lambda: V.tensor_tensor(
                        out=dens4, in0=pN2[:, 0:NS * 16].rearrange("p (k b g) -> p b k g", k=4, b=NS), in1=dens4,
                        op=ALU.add), reads=[pN2, dens], writes=[dens])
                    fw.op(dve, lambda: V.tensor_tensor(out=dens[:], in0=dens[:],
                                                       in1=esink[:].unsqueeze(1).broadcast_to([128, NS, 16]), op=ALU.add),
                          reads=[dens, esink], writes=[dens])
                    fw.op(dve, lambda: V.reciprocal(out=dens[:], in_=dens[:]), reads=[dens], writes=[dens])
                    fw.op(dve, lambda: V.tensor_tensor(out=osb[:], in0=osb[:], in1=dens[:], op=ALU.mult),
                          reads=[osb, dens], writes=[osb])
                    ckpt('s_new')
                    for r in range(2):
                        rows = slice(64 * r, 64 * r + 64)
                        fw.op(dve, lambda: V.tensor_tensor(
                            out=ogT[rows, :, 0:NS],
                            in0=osb[rows].rearrange("p b (m r) -> p r m b", r=2)[:, r],
                            in1=sgT[rows], op=ALU.mult), reads=[osb, sgT], writes=[ogT])
                    layer_b_tail(NT, 128, hb, ysm[:, :], NS)

        try:
            body()
        except StopBuild:
            pass
        for h in fw.engs + fw.dma_holders:
            if h is not sp and h.cnt > 0 and sp.seen.get(h, 0) < h.cnt:
                sp.h.wait_ge(h.sem, h.cnt)
                sp.seen[h] = h.cnt
    return nc


def _consts():
    pos = np.concatenate([np.arange(SEQ), np.full(128, 8192)]).astype(np.float32)
    inv = (np.float32(500000.0) ** (-np.arange(0, 16, 2, dtype=np.float32) / np.float32(16))).astype(np.float32)
    ang = (pos[:, None] * inv[None, :]).astype(np.float32)
    rope = np.concatenate([np.cos(ang), np.sin(ang)], axis=1).astype(np.float32)
    ident = np.eye(128, dtype=np.float32)
    j = np.arange(128)[:, None]
    i = np.arange(128)[None, :]
    cmask = (j <= i).astype(np.float32)
    mprev = np.where(j >= i, 0.0, NEG).astype(np.float32)
    mcur = np.where(j <= i, 0.0, NEG).astype(np.float32)
    return dict(rope=rope, ident=ident, cmask=cmask, mprev=np.tile(mprev, (1, 4)), mcur=np.tile(mcur, (1, 4)))


_NC_CACHE = {}


def kernel(x_prompt, x_sample, cache_k, cache_v, norm_a, w_in_a, v_norm_a, w_s_a, b_s_a, w_out_a,
           kv_norm, w_kv, norm_b, w_in_b, sinks_b, w_out_b, final_norm):
    f = lambda a: np.ascontiguousarray(np.asarray(a, dtype=np.float32))
    colT = lambda v, k: f(np.asarray(v).reshape(k, 128).T)
    shared = dict(
        w_in_a=f(np.asarray(w_in_a)[0]), w_out_a=f(np.asarray(w_out_a)[0]), w_kv=f(w_kv),
        w_in_b=f(np.asarray(w_in_b)[0]), w_out_b=f(np.asarray(w_out_b)[0]),
        gaT=colT(norm_a, 8), gvT=colT(v_norm_a, 16), gkvT=colT(kv_norm, 8), gbT=colT(norm_b, 8),
        gv_row=f(np.asarray(v_norm_a).reshape(-1)), gf_row=f(np.asarray(final_norm).reshape(-1)),
        wsT=f(np.transpose(np.asarray(w_s_a)[0], (2, 0, 1)).reshape(128, 1024)),
        w00=f(np.asarray(w_s_a)[0, :, 0, 0]), bs=f(np.asarray(b_s_a)[0].reshape(-1)),
        bs0=f(np.asarray(b_s_a)[0, :, 0]), sinks=f(np.asarray(sinks_b).reshape(-1)),
    )
    shared.update(_consts())
    xp = np.asarray(x_prompt, dtype=np.float32)
    xs = np.asarray(x_sample, dtype=np.float32).reshape(128, D)
    ck = np.asarray(cache_k, dtype=np.float32).reshape(128, 128, 256)
    cv = np.asarray(cache_v, dtype=np.float32).reshape(128, 128, 256)
    in_maps = []
    for c in range(NCORES):
        m = dict(shared)
        m["xp"] = f(xp[c])
        m["xsm"] = f(xs[c * NS:(c + 1) * NS])
        m["ck"] = f(ck[c * NS:(c + 1) * NS])
        m["cv"] = f(cv[c * NS:(c + 1) * NS])
        in_maps.append(m)
    if "nc" not in _NC_CACHE:
        _NC_CACHE["nc"] = build_program()
    res = run_bass_kernel_spmd(_NC_CACHE["nc"], in_maps, core_ids=list(range(NCORES)))
    R = res.results
    y_prompt = np.stack([R[c]["yp"] for c in range(NCORES)]).astype(np.float32)
    y_sample = np.concatenate([R[c]["ysm"] for c in range(NCORES)]).reshape(128, 1, D).astype(np.float32)
    nk_p = np.stack([R[c]["nkp"] for c in range(NCORES)]).reshape(8, 128, 4, 64).astype(np.float32)
    nv_p = np.stack([R[c]["nvp"] for c in range(NCORES)]).reshape(8, 128, 4, 64).astype(np.float32)
    nk_s = np.concatenate([R[c]["nks"] for c in range(NCORES)]).reshape(128, 128, 4, 64).astype(np.float32)
    nv_s = np.concatenate([R[c]["nvs"] for c in range(NCORES)]).reshape(128, 128, 4, 64).astype(np.float32)
    nav = np.concatenate([R[c]["nav"] for c in range(NCORES)]).reshape(1, 128, 1, AW).astype(np.float32)
    return (y_prompt, y_sample, nk_p, nv_p, nk_s, nv_s, nav)
```
